# Optimizing a Trainium2 kernel written in Bass

```python
import jax
import jax.numpy as jnp
from jax import lax
import numpy as np

D_MODEL = 2048
BATCH = 4
SEQ = 2048
DEPTH = 4

CHUNK = 64
N_A_LAYERS = DEPTH // 2
N_B_LAYERS = DEPTH - N_A_LAYERS
S5_GROUP = 16
S5_GROUPS = D_MODEL // S5_GROUP
S5_STATE = 64
DT_MIN = 1e-3
DT_MAX = 1e-1
B_HEADS = 16
B_HEAD_DIM = D_MODEL // B_HEADS
LEFT_CHUNKS = 8
BAND = LEFT_CHUNKS + 1
MAX_REL = 256
N_REL = MAX_REL + CHUNK
N_MEM = 256
MEM_HEADS = 4
MEM_HEAD_DIM = 128
MEM_DIM = MEM_HEADS * MEM_HEAD_DIM
D_FF = 5632
CONV_W = 3
EPS = 1e-6
NEG_INF = -1e30

kernel_name = "s5_chunkattn_yoco_hybrid_trunk"


def rmsnorm(x, g):
    xf = x.astype(jnp.float32)
    y = xf * lax.rsqrt(jnp.mean(xf * xf, axis=-1, keepdims=True) + EPS)
    return (y * g.astype(jnp.float32)).astype(x.dtype)


def _ssm_combine(e1, e2):
    a1r, a1i, b1r, b1i = e1
    a2r, a2i, b2r, b2i = e2
    ar = a2r * a1r - a2i * a1i
    ai = a2r * a1i + a2i * a1r
    br = a2r * b1r - a2i * b1i + b2r
    bi = a2r * b1i + a2i * b1r + b2i
    return ar, ai, br, bi


def s5_ssm(u, lam_re, lam_im, log_dt, b_re, b_im, c_re, c_im, d_skip):
    f32 = jnp.float32
    bsz, seq, _ = u.shape
    uf = u.astype(f32).reshape(bsz, seq, S5_GROUPS, S5_GROUP)
    lr = lam_re.astype(f32)
    li = lam_im.astype(f32)
    dt = jnp.exp(log_dt.astype(f32))[:, None]
    mag = jnp.exp(lr * dt)
    ang = li * dt
    ab_re = mag * jnp.cos(ang)
    ab_im = mag * jnp.sin(ang)
    den = lr * lr + li * li
    nr = ab_re - 1.0
    f_re = (nr * lr + ab_im * li) / den
    f_im = (ab_im * lr - nr * li) / den
    br = b_re.astype(f32)
    bi = b_im.astype(f32)
    bb_re = f_re[..., None] * br - f_im[..., None] * bi
    bb_im = f_re[..., None] * bi + f_im[..., None] * br
    bu_re = jnp.einsum("bsgh,gph->sbgp", uf, bb_re)
    bu_im = jnp.einsum("bsgh,gph->sbgp", uf, bb_im)
    a_re = jnp.broadcast_to(ab_re, (seq, 1) + ab_re.shape)
    a_im = jnp.broadcast_to(ab_im, (seq, 1) + ab_im.shape)
    _, _, x_re, x_im = lax.associative_scan(_ssm_combine, (a_re, a_im, bu_re, bu_im), axis=0)
    y = (jnp.einsum("sbgp,ghp->bsgh", x_re, c_re.astype(f32))
         - jnp.einsum("sbgp,ghp->bsgh", x_im, c_im.astype(f32)))
    y = y + d_skip.astype(f32).reshape(S5_GROUPS, S5_GROUP) * uf
    return y.reshape(bsz, seq, D_MODEL).astype(u.dtype)


def mixer_a(h, w_in, lam_re, lam_im, log_dt, b_re, b_im, c_re, c_im, d_skip, w_glu):
    u = h @ w_in
    y = jax.nn.gelu(s5_ssm(u, lam_re, lam_im, log_dt, b_re, b_im, c_re, c_im, d_skip))
    val, gate = jnp.split(y @ w_glu, 2, axis=-1)
    return val * jax.nn.sigmoid(gate)


def rel_bias_index():
    qi = np.arange(CHUNK)[:, None, None]
    slot = np.arange(BAND)[None, :, None]
    kj = np.arange(CHUNK)[None, None, :]
    dist = (LEFT_CHUNKS - slot) * CHUNK + qi - kj
    idx = np.clip(dist, -(CHUNK - 1), MAX_REL) + (CHUNK - 1)
    return idx.reshape(CHUNK, BAND * CHUNK)


def band_gather(t, n_chunks):
    bsz = t.shape[0]
    tc = t.reshape(bsz, n_chunks, CHUNK, B_HEADS, B_HEAD_DIM)
    tp = jnp.pad(tc, ((0, 0), (LEFT_CHUNKS, 0), (0, 0), (0, 0), (0, 0)))
    return jnp.concatenate([tp[:, i:i + n_chunks] for i in range(BAND)], axis=2)


def mixer_b(h, w_q, rel_bias, w_o, k_band, v_band, band_valid):
    bsz, seq, _ = h.shape
    n_chunks = seq // CHUNK
    q = (h @ w_q).reshape(bsz, n_chunks, CHUNK, B_HEADS, B_HEAD_DIM)
    scale = B_HEAD_DIM ** -0.5
    s = jnp.einsum("bnqhd,bnkhd->bnhqk", q.astype(jnp.float32), k_band.astype(jnp.float32)) * scale
    bias = rel_bias.astype(jnp.float32)[:, rel_bias_index()]
    s = s + bias[None, None]
    s = jnp.where(band_valid[None, :, None, None, :], s, NEG_INF)
    p = jax.nn.softmax(s, axis=-1).astype(v_band.dtype)
    o = jnp.einsum("bnhqk,bnkhd->bnqhd", p, v_band).reshape(bsz, seq, D_MODEL)
    return o @ w_o


def mem_attention(h, mem_n, w_q, w_kv, w_o):
    bsz, seq, _ = h.shape
    q = (h @ w_q).reshape(bsz, seq, MEM_HEADS, MEM_HEAD_DIM)
    k, v = jnp.split(mem_n @ w_kv, 2, axis=-1)
    k = k.reshape(bsz, -1, MEM_HEADS, MEM_HEAD_DIM)
    v = v.reshape(bsz, -1, MEM_HEADS, MEM_HEAD_DIM)
    s = jnp.einsum("bshd,bmhd->bhsm", q.astype(jnp.float32), k.astype(jnp.float32)) * (MEM_HEAD_DIM ** -0.5)
    p = jax.nn.softmax(s, axis=-1).astype(v.dtype)
    o = jnp.einsum("bhsm,bmhd->bshd", p, v).reshape(bsz, seq, MEM_DIM)
    return o @ w_o


def conv_ffn(h, w_up, conv_w, conv_b, w_down):
    seq = h.shape[1]
    up = h @ w_up
    upp = jnp.pad(up, ((0, 0), (CONV_W - 1, 0), (0, 0)))
    acc = conv_b
    for k in range(CONV_W):
        acc = acc + upp[:, k:k + seq] * conv_w[k]
    val, gate = jnp.split(acc, 2, axis=-1)
    return (val * jax.nn.silu(gate)) @ w_down


def setup_inputs(seed: int = 0) -> dict:
    key = jax.random.key(seed)
    ks = jax.random.split(key, 40)
    f32 = jnp.float32
    D, G, P, H = D_MODEL, S5_GROUPS, S5_STATE, S5_GROUP

    def nrm(k, shape, scale):
        return jax.random.normal(k, shape, f32) * scale

    def gain(k, shape):
        return 1.0 + 0.05 * jax.random.normal(k, shape, f32)

    lam_im0 = jnp.pi * jnp.arange(P, dtype=f32)
    inp = {
        "x": nrm(ks[0], (BATCH, SEQ, D), 1.0),
        "mem": nrm(ks[1], (BATCH, N_MEM, D), 1.0),
        "norm_mix": gain(ks[2], (DEPTH, 2, D)),
        "norm_mem": gain(ks[3], (DEPTH, 2, D)),
        "norm_ffn": gain(ks[4], (DEPTH, 2, D)),
        "mem_in_norm": gain(ks[5], (D,)),
        "a_w_in": nrm(ks[6], (N_A_LAYERS, D, D), D ** -0.5),
        "a_lam_re": -0.5 + 0.01 * jax.random.normal(ks[7], (N_A_LAYERS, G, P), f32),
        "a_lam_im": lam_im0 + 0.01 * jax.random.normal(ks[8], (N_A_LAYERS, G, P), f32),
        "a_log_dt": jax.random.uniform(ks[9], (N_A_LAYERS, G), f32, float(np.log(DT_MIN)), float(np.log(DT_MAX))),
        "a_b_re": nrm(ks[10], (N_A_LAYERS, G, P, H), (2 * H) ** -0.5),
        "a_b_im": nrm(ks[11], (N_A_LAYERS, G, P, H), (2 * H) ** -0.5),
        "a_c_re": nrm(ks[12], (N_A_LAYERS, G, H, P), P ** -0.5),
        "a_c_im": nrm(ks[13], (N_A_LAYERS, G, H, P), P ** -0.5),
        "a_d": nrm(ks[14], (N_A_LAYERS, D), 1.0),
        "a_w_glu": nrm(ks[15], (N_A_LAYERS, D, 2 * D), D ** -0.5),
        "kv_norm": gain(ks[16], (D,)),
        "w_k": nrm(ks[17], (D, D), D ** -0.5),
        "w_v": nrm(ks[18], (D, D), D ** -0.5),
        "b_w_q": nrm(ks[19], (N_B_LAYERS, D, D), D ** -0.5),
        "b_rel_bias": nrm(ks[20], (N_B_LAYERS, B_HEADS, N_REL), 0.1),
        "b_w_o": nrm(ks[21], (N_B_LAYERS, D, D), D ** -0.5),
        "m_w_q": nrm(ks[22], (DEPTH, D, MEM_DIM), D ** -0.5),
        "m_w_kv": nrm(ks[23], (DEPTH, D, 2 * MEM_DIM), D ** -0.5),
        "m_w_o": nrm(ks[24], (DEPTH, MEM_DIM, D), MEM_DIM ** -0.5),
        "f_w_up": nrm(ks[25], (DEPTH, D, 2 * D_FF), D ** -0.5),
        "f_conv_w": nrm(ks[26], (DEPTH, CONV_W, 2 * D_FF), CONV_W ** -0.5),
        "f_conv_b": nrm(ks[27], (DEPTH, 2 * D_FF), 0.01),
        "f_w_down": nrm(ks[28], (DEPTH, D_FF, D), D_FF ** -0.5),
    }
    return inp


def reference(x, mem, norm_mix, norm_mem, norm_ffn, mem_in_norm,
              a_w_in, a_lam_re, a_lam_im, a_log_dt, a_b_re, a_b_im, a_c_re, a_c_im, a_d, a_w_glu,
              kv_norm, w_k, w_v, b_w_q, b_rel_bias, b_w_o,
              m_w_q, m_w_kv, m_w_o, f_w_up, f_conv_w, f_conv_b, f_w_down):
    bsz, seq, _ = x.shape
    n_chunks = seq // CHUNK
    slot_chunk = np.arange(n_chunks)[:, None] - LEFT_CHUNKS + np.arange(BAND)[None, :]
    band_valid = jnp.asarray(np.repeat(slot_chunk >= 0, CHUNK, axis=1))
    mem_n = rmsnorm(mem, mem_in_norm)
    k_band = None
    v_band = None
    for l in range(DEPTH):
        if l == N_A_LAYERS:
            hk = rmsnorm(x, kv_norm)
            k = (hk @ w_k).reshape(bsz, seq, B_HEADS, B_HEAD_DIM)
            v = (hk @ w_v).reshape(bsz, seq, B_HEADS, B_HEAD_DIM)
            k_band = band_gather(k, n_chunks)
            v_band = band_gather(v, n_chunks)
        h = rmsnorm(x, norm_mix[l, 0])
        if l < N_A_LAYERS:
            m = mixer_a(h, a_w_in[l], a_lam_re[l], a_lam_im[l], a_log_dt[l], a_b_re[l], a_b_im[l],
                        a_c_re[l], a_c_im[l], a_d[l], a_w_glu[l])
        else:
            j = l - N_A_LAYERS
            m = mixer_b(h, b_w_q[j], b_rel_bias[j], b_w_o[j], k_band, v_band, band_valid)
        x = x + rmsnorm(m, norm_mix[l, 1])
        c = mem_attention(rmsnorm(x, norm_mem[l, 0]), mem_n, m_w_q[l], m_w_kv[l], m_w_o[l])
        x = x + rmsnorm(c, norm_mem[l, 1])
        f = conv_ffn(rmsnorm(x, norm_ffn[l, 0]), f_w_up[l], f_conv_w[l], f_conv_b[l], f_w_down[l])
        x = x + rmsnorm(f, norm_ffn[l, 1])
    return x
```

```python
import math
import os
CUT = int(os.environ.get('KCUT', '99'))
import numpy as np
import concourse.bass as bass
import concourse.mybir as mybir
from concourse.bass_utils import run_bass_kernel_spmd

F32 = mybir.dt.float32
BF16 = mybir.dt.bfloat16
AF = mybir.ActivationFunctionType
ALU = mybir.AluOpType
AX = mybir.AxisListType

D = 2048
NTOK = 1024
KT = 16
DEPTH = 4
NA = 2
DFF = 5632
NFT = 44
EPS = 1e-6
NMEM = 256
PI = math.pi

G_MIX0, G_MIX1, G_MEM0, G_MEM1, G_FFN0, G_FFN1 = range(6)


def gcol(l, which):
    return (l * 6 + which) * 16


G_MEMIN = 24 * 16
G_KV = 25 * 16
CONV0 = 26 * 16
DD0 = CONV0 + 4 * 352
ISSEC = DD0 + 256
NPRM = ISSEC + 8

SB_BASE = 16640
XT_OFF = SB_BASE
P1_OFF = SB_BASE + 65536
P2_OFF = SB_BASE + 98304
T_OFF = SB_BASE + 143360
PRM_OFF = SB_BASE + 176128
MEMN_OFF = PRM_OFF + 8384
MISC_OFF = MEMN_OFF + 8192


class KB:
    NDMA = 24

    def __init__(self):
        nc = bass.Bass("TRN2", target_bir_lowering=False)
        self.nc = nc
        self.eng = {"pe": nc.tensor, "act": nc.scalar, "dve": nc.vector, "pool": nc.gpsimd, "sp": nc.sync}
        self.prog = {e: nc.alloc_semaphore("prog_" + e) for e in self.eng}
        self.cnt = {e: 0 for e in self.eng}
        self.waited = {e: {} for e in self.eng}
        self.lastw = {}
        self.readers = {}
        self.dsem = [nc.alloc_semaphore("dsem%d" % i) for i in range(self.NDMA)]
        self.dval = [0] * self.NDMA
        self.drr = 0
        self.cc_sem = nc.alloc_semaphore("cc_sem")
        self.cc_val = 0
        self._misc = MISC_OFF
        self._names = 0

    def sb(self, shape, dt, off):
        self._names += 1
        return self.nc.alloc_sbuf_tensor_at("t%d" % self._names, list(shape), dt, offset=off).ap()

    def misc(self, shape, dt):
        n = int(np.prod(shape[1:])) * (4 if dt == F32 else 2)
        n = (n + 31) // 32 * 32
        off = self._misc
        self._misc += n
        assert self._misc <= 229376, self._misc
        return self.sb(shape, dt, off)

    def _wait(self, e, tok):
        kind, ident, val = tok
        if kind == "eng":
            if ident == e and e == "pe":
                return
            sem = self.prog[ident]
        else:
            sem = self.dsem[ident]
        key = (kind, ident)
        if self.waited[e].get(key, 0) >= val:
            return
        self.eng[e].wait_ge(sem, val)
        self.waited[e][key] = val

    def _deps(self, e, r, w):
        for k in r:
            t = self.lastw.get(k)
            if t is not None:
                self._wait(e, t)
        for k in w:
            t = self.lastw.get(k)
            if t is not None:
                self._wait(e, t)
            for t in self.readers.get(k, {}).values():
                self._wait(e, t)

    def _record(self, tok, r, w):
        src = (tok[0], tok[1])
        for k in r:
            self.readers.setdefault(k, {})[src] = tok
        for k in w:
            self.lastw[k] = tok
            self.readers[k] = {}

    def op(self, e, fn, r=(), w=()):
        self._deps(e, r, w)
        ins = fn(self.eng[e])
        self.cnt[e] += 1
        ins.then_inc(self.prog[e], 1)
        self._record(("eng", e, self.cnt[e]), r, w)
        return ins

    def dma(self, q, out, in_, r=(), w=(), **kw):
        self._deps(q, r, w)
        i = self.drr
        self.drr = (self.drr + 1) % self.NDMA
        if self.dval[i] > 0:
            self._wait(q, ("dma", i, self.dval[i]))
        self.dval[i] += 16
        ins = self.eng[q].dma_start(out=out, in_=in_, **kw)
        ins.then_inc(self.dsem[i], 16)
        self._record(("dma", i, self.dval[i]), r, w)
        return ins

    def allgather_pairs(self, in_ap, out_ap, r=(), w=()):
        self._deps("pool", r, w)
        g = self.eng["pool"]
        g.collective_compute("AllGather", ALU.bypass, replica_groups=[[0, 1], [2, 3], [4, 5], [6, 7]],
                             ins=[in_ap], outs=[out_ap]).then_inc(self.cc_sem)
        self.cc_val += 1
        g.wait_ge(self.cc_sem, self.cc_val)
        self.op("pool", lambda e: e.memset(self.dummy, 0.0), r=r, w=list(w) + ["dummy"])

    def barrier(self):
        for e in self.eng:
            for o in self.eng:
                if o != e and self.cnt[o] > 0:
                    self._wait(e, ("eng", o, self.cnt[o]))
            for i in range(self.NDMA):
                if self.dval[i] > 0:
                    self._wait(e, ("dma", i, self.dval[i]))
        self.lastw = {}
        self.readers = {}


def bc(ap, shape):
    return ap.broadcast_to(list(shape))


class Prog(KB):
    def __init__(self, stages, dumps=()):
        super().__init__()
        nc = self.nc
        self.stages = stages
        self.dumps = dict()
        self.in_names = []
        need = set()
        for st in stages:
            if st == "memprep":
                continue
            need.add(st if isinstance(st, tuple) else (st,))

        def di(name, shape, dt=F32):
            self.in_names.append(name)
            return nc.dram_tensor(name, list(shape), dt, kind="ExternalInput").ap()

        def dil(kind, name, shape, layers):
            return {l: di("%s_%d" % (name, l), shape) for l in layers if (kind, l) in need}

        self.d_xT = di("xT", [D, NTOK])
        self.d_memT = di("memT", [D, NMEM])
        self.d_prm = di("prm", [128, NPRM])
        self.d_cst = di("cst", [128, 288])
        self.d_hmask = di("hmask", [1, 1152])
        self.d_maskT = di("maskT", [128, 640])
        self.d_selc = di("selc", [128, 1920 + 2160])
        self.d_s5lam = dil("mixa", "s5lam", [128, 3, 64], range(NA))
        self.d_s5bc = dil("mixa", "s5bc", [4, 128, 64, 16], range(NA))
        self.d_win = dil("mixa", "a_w_in", [16, 128, 16, 128], range(NA))
        self.d_wglu = dil("mixa", "a_w_glu", [16, 128, 16, 256], range(NA))
        if ("kv",) in need:
            self.d_wk = di("w_k", [16, 128, 16, 128])
            self.d_wv = di("w_v", [8, 128, 16, 256])
        self.d_bwq = dil("mixb", "b_w_q", [16, 128, 16, 128], range(NA, DEPTH))
        self.d_bwo = dil("mixb", "b_w_o", [16, 128, 16, 128], range(NA, DEPTH))
        self.d_bias = dil("mixb", "b_bias", [16, 128, 640], range(NA, DEPTH))
        self.d_mwq = dil("mem", "m_w_q", [4, 128, 16, 128], range(DEPTH))
        self.d_mwk = dil("mem", "m_w_k", [4, 128, 16, 128], range(DEPTH))
        self.d_mwv = dil("mem", "m_w_v", [2, 128, 16, 256], range(DEPTH))
        self.d_mwo = dil("mem", "m_w_o", [16, 128, 4, 128], range(DEPTH))
        self.d_wup = dil("ffn", "f_w_up", [NFT, 128, 16, 256], range(DEPTH))
        self.d_wdn = dil("ffn", "f_w_down", [16, 2, 128, 22, 128], range(DEPTH))
        self.d_out = nc.dram_tensor("outT", [D, NTOK], F32, kind="ExternalOutput").ap()
        for name, shape in dumps:
            self.dumps[name] = nc.dram_tensor("dbg_" + name, list(shape), F32, kind="ExternalOutput").ap()
        dt_ = lambda name, shape, dt: nc.dram_tensor(name, list(shape), dt).ap()
        self.kbuf = dt_("kbuf", [D, 1536], BF16)
        self.vbuf = dt_("vbuf", [1536, D], BF16)
        self.kx_in = dt_("kx_in", [D, 512], BF16)
        self.kx_out = dt_("kx_out", [2 * D, 512], BF16)
        self.vx_in = dt_("vx_in", [512, D], BF16)
        self.vx_out = dt_("vx_out", [1024, D], BF16)
        self.hh_in = dt_("hh_in", [128, 32], BF16)
        self.hh_out = dt_("hh_out", [256, 32], BF16)
        self.xs_in = dt_("xs_in", [128, 128], F32)
        self.xs_out = dt_("xs_out", [256, 128], F32)
        self.tb_wo = dt_("tb_wo", [8, 128, 2, 8, 128], BF16)
        self.tb_toep = dt_("tb_toep", [8, 128, 16, 128], BF16)

        self.xT = self.sb([128, KT, NTOK], F32, XT_OFF)
        self.P1 = self.sb([128, 16384], BF16, P1_OFF)
        self.P2 = self.sb([128, 22528], BF16, P2_OFF)
        self.P2f = self.sb([128, 11264], F32, P2_OFF)
        self.Tb = self.sb([128, 16384], BF16, T_OFF)
        self.Tf = self.sb([128, 8192], F32, T_OFF)
        self.prm = self.sb([128, NPRM], F32, PRM_OFF)
        self.memn = self.sb([128, KT, NMEM], BF16, MEMN_OFF)
        self.rstd = self.misc([128, 512], F32)
        self.sqt = [self.misc([128, 512], BF16) for _ in range(2)]
        self.ctmp = [self.misc([128, 512], F32) for _ in range(3)]
        self.ident_bf = self.misc([128, 128], BF16)
        self.ones_bf = self.misc([128, 128], BF16)
        self.cst = self.misc([128, 288], F32)
        self.ident_f = self.cst[:, 0:128]
        self.cmask = self.cst[:, 128:256]
        self.kvph = self.cst[:, 256:288]
        self.hh = self.misc([128, 32], BF16)
        self.hh0 = self.misc([128, 32], BF16)
        self.convhalo = self.misc([128, 88, 2], F32)
        self.small = self.misc([128, 64], F32)
        self.maskrow = self.misc([1, 1152], BF16)
        self.dummy = self.misc([128, 8], F32)
        self.xs5 = [self.misc([128, 64, 2], F32) for _ in range(4)]
        self.xs5_ar = self.misc([128, 64], F32)
        self.xs5_ai2 = self.misc([128, 64, 2], F32)
        self.negpi = self.misc([128, 1], F32)
        self.s5f = self.misc([128, 128], F32)
        self.s5i = self.misc([128, 128], mybir.dt.int32)
        self.psall = nc.alloc_psum_tensor("psall", [128, 4096], F32).ap()
        self.ps2 = [self.psall[:, 1024 * i:1024 * i + 1024] for i in range(4)]
        self.sq_i = 0
        self.wrr = 0
        self.build()

    def ps(self, i):
        return self.ps2[i // 2][:, (i % 2) * 512:(i % 2) * 512 + 512]

    def pskey(self, i):
        return ("ps", i)

    def wload(self, dram_ap):
        s = self.wrr
        self.wrr = (self.wrr + 1) % 4
        a, b = dram_ap.shape[1], dram_ap.shape[2]
        assert a * b <= 4096
        slot = self.Tb[:, s * 4096:s * 4096 + a * b].rearrange("p (a b) -> p a b", a=a)
        self.dma("pool", slot, dram_ap, w=[("w", s)])
        return slot, ("w", s)

    def dump(self, name, ap, keys):
        if name in self.dumps:
            self.barrier()
            self.dma("pool", self.dumps[name], ap, w=[("dump", name)])
            self.barrier()

    def gain(self, col, k):
        return self.prm[:, col + k:col + k + 1]

    def ss_acc(self, src, keys, first, last, n):
        i = self.sq_i
        self.sq_i ^= 1
        sq = self.sqt[i][:, :n]
        self.op("act", lambda e: e.activation(out=sq, in_=src, func=AF.Square), r=keys, w=[("sqt", i)] + [k_ for k_ in keys if k_[0] == "ps"])
        self.op("pe", lambda e: e.matmul(self.ps(7)[:, :n], lhsT=self.ones_bf, rhs=sq, start=first, stop=last),
                r=[("sqt", i), "ones"], w=[self.pskey(7)])

    def rstd_fin(self, n, dim=D):
        r = self.rstd[:, :n]
        self.op("act", lambda e: e.activation(out=r, in_=self.ps(7)[:, :n], func=AF.Sqrt, bias=self.epsc, scale=1.0 / dim),
                r=[self.pskey(7), "cst"], w=["rstd"])
        self.op("dve", lambda e: e.reciprocal(out=r, in_=r), r=["rstd"], w=["rstd"])

    def norm_pre(self, c0, n, gcol_, out_fn, out_keyfn, xkeys):
        for k in range(KT):
            self.ss_acc(self.xT[:, k, c0:c0 + n], [xkeys(k)], k == 0, k == KT - 1, n)
        self.rstd_fin(n)
        for k in range(KT):
            self.op("dve", lambda e: e.scalar_tensor_tensor(out=out_fn(k), in0=self.xT[:, k, c0:c0 + n], scalar=self.gain(gcol_, k),
                                                            in1=self.rstd[:, :n], op0=ALU.mult, op1=ALU.mult),
                    r=[xkeys(k), "rstd", "prm"], w=[out_keyfn(k)])

    def post_apply(self, c0, n, gcol_, m_fn, m_keyfn, xkeys):
        self.rstd_fin(n)
        for k in range(KT):
            t = self.ctmp[k % 3][:, :n]
            self.op("pool", lambda e: e.tensor_tensor(out=t, in0=m_fn(k), in1=self.rstd[:, :n], op=ALU.mult),
                    r=[m_keyfn(k), "rstd"], w=[("ctmp", k % 3)])
            xk = self.xT[:, k, c0:c0 + n]
            self.op("dve", lambda e: e.scalar_tensor_tensor(out=xk, in0=t, scalar=self.gain(gcol_, k), in1=xk,
                                                            op0=ALU.mult, op1=ALU.add),
                    r=[("ctmp", k % 3), "prm", xkeys(k)], w=[xkeys(k)])

    def attn_tile(self, slot, qT, qkeys, kT, kkeys, nk, v_fn, vkeys, out_ap, out_keys, bias=None, bkeys=(), mask=None):
        PA = self.psall
        if nk > 256:
            base = 1024 * slot
            S = PA[:, base:base + nk]
            TP = PA[:, base + 640:base + 640 + nk // 2].bitcast(BF16)
            O = PA[:, base + 512:base + 640]
        else:
            base = 512 * slot
            S = PA[:, base:base + nk]
            TP = PA[:, base + 256:base + 256 + nk // 2].bitcast(BF16)
            O = PA[:, base + 384:base + 512]
        sk = [("aslot", slot)]
        chunks = [(0, min(nk, 512))] + ([(512, nk)] if nk > 512 else [])
        i = slot
        mx = self.small[:, 4 * i:4 * i + 1]
        sm = self.small[:, 4 * i + 1:4 * i + 2]
        rs = self.small[:, 4 * i + 2:4 * i + 3]
        smk = ("small", i)
        Pf = self.Pf[i][:, :nk]
        PT = self.PT[i][:, :nk]
        nkt = nk // 128
        qkeys, kkeys, bkeys, vkeys, out_keys = list(qkeys), list(kkeys), list(bkeys), list(vkeys), list(out_keys)

        def st_scores():
            for (a, b) in chunks:
                last_plain = bias is None and mask is None
                self.op("pe", lambda e: e.matmul(S[:, a:b], lhsT=qT, rhs=kT[:, a:b], start=True, stop=last_plain),
                        r=qkeys + kkeys, w=sk)
                if bias is not None:
                    self.op("pe", lambda e: e.matmul(S[:, a:b], lhsT=self.ident_bf, rhs=bias[:, a:b], start=False, stop=(mask is None)),
                            r=bkeys + ["ident"], w=sk)
                if mask is not None:
                    self.op("pe", lambda e: e.matmul(S[:, a:b], lhsT=self.ones_bf[0:1, :], rhs=mask[:, a:b], start=False, stop=True),
                            r=["ones", "maskrow"], w=sk)

        def st_max():
            self.op("dve", lambda e: e.reduce_max(out=mx, in_=S, axis=AX.X), r=[], w=sk + [smk])
            self.op("dve", lambda e: e.tensor_scalar(out=mx, in0=mx, scalar1=-1.0, scalar2=None, op0=ALU.mult), r=[smk], w=[smk])

        def st_exp():
            self.op("act", lambda e: e.activation(out=Pf, in_=S, func=AF.Exp, bias=mx, scale=1.0), r=[smk], w=sk + [("Pf", i)])

        def st_norm():
            self.op("dve", lambda e: e.reduce_sum(out=sm, in_=Pf, axis=AX.X), r=[("Pf", i)], w=[smk])
            self.op("dve", lambda e: e.reciprocal(out=rs, in_=sm), r=[smk], w=[smk])
            self.op("dve", lambda e: e.tensor_scalar(out=Pf, in0=Pf, scalar1=rs, scalar2=None, op0=ALU.mult), r=[smk], w=[("Pf", i)])

        def st_tr():
            for kt in range(nkt):
                self.op("pe", lambda e: e.transpose(out=TP[:, kt * 128:(kt + 1) * 128], in_=Pf[:, kt * 128:(kt + 1) * 128], identity=self.ident_bf),
                        r=[("Pf", i), "ident"], w=sk)

        def st_ptcopy():
            self.op("act", lambda e: e.copy(out=PT, in_=TP), r=[], w=sk + [("PT", i)])

        def st_pv():
            for kt in range(nkt):
                self.op("pe", lambda e: e.matmul(O, lhsT=v_fn(kt), rhs=PT[:, kt * 128:(kt + 1) * 128], start=(kt == 0), stop=(kt == nkt - 1)),
                        r=[("PT", i)] + vkeys, w=sk)

        def st_out():
            self.op("act", lambda e: e.copy(out=out_ap, in_=O), r=[], w=sk + out_keys)

        return [st_scores, st_max, st_exp, st_norm, st_tr, st_ptcopy, st_pv, st_out]

    def attn_run(self, tiles, groups):
        nst = len(groups)
        live = {}
        for step in range(len(tiles) + nst - 1):
            for sidx in range(nst - 1, -1, -1):
                t = step - sidx
                if 0 <= t < len(tiles):
                    if sidx == 0:
                        live[t] = tiles[t]()
                    for prim in groups[sidx]:
                        live[t][prim]()
                    if sidx == nst - 1:
                        del live[t]

    def alloc_attn_bufs(self, off_bytes):
        o = off_bytes
        self.Pb = []
        self.PT = []
        for i in range(2):
            self.Pb.append(self.sb([128, 640], BF16, T_OFF + o)); o += 1280
            self.PT.append(self.sb([128, 640], BF16, T_OFF + o)); o += 1280
        self.Pf = [self.sb([128, 640], F32, T_OFF + o), self.sb([128, 640], F32, T_OFF + o + 2560)]
        o += 5120
        return o

    def setup(self):
        self.dma("sp", self.prm, self.d_prm, w=["prm"])
        self.dma("sp", self.cst, self.d_cst, w=["cst"])
        self.dma("pool", self.maskrow, self.d_hmask, w=["maskrow"])
        for k in range(KT):
            self.dma("sp", self.xT[:, k, :], self.d_xT[k * 128:(k + 1) * 128, :], w=[("x", k, 0), ("x", k, 1)])
        self.op("dve", lambda e: e.tensor_copy(out=self.ident_bf, in_=self.ident_f), r=["cst"], w=["ident"])
        self.op("dve", lambda e: e.memset(self.ones_bf, 1.0), w=["ones"])
        self.epsc = self.misc([128, 1], F32)
        self.op("dve", lambda e: e.memset(self.epsc, EPS), w=["cst2"])
        self.op("dve", lambda e: e.memset(self.negpi, -PI), w=["cst3"])
        self.barrier()

    def xk(self, k, b=None):
        return ("x", k, b)

    def mem_prepare(self):
        st = self.P2f[:, 0:KT * NMEM].rearrange("p (k t) -> p k t", k=KT)
        self.dma("sp", st, self.d_memT.rearrange("(k p) t -> p k t", p=128), w=["memst"])
        for k in range(KT):
            self.ss_acc(st[:, k, :], ["memst"], k == 0, k == KT - 1, NMEM)
        self.rstd_fin(NMEM)
        for k in range(KT):
            self.op("dve", lambda e: e.scalar_tensor_tensor(out=self.memn[:, k, :], in0=st[:, k, :], scalar=self.gain(G_MEMIN, k),
                                                            in1=self.rstd[:, :NMEM], op0=ALU.mult, op1=ALU.mult),
                    r=["memst", "rstd", "prm"], w=["memn"])
        self.barrier()

    def mem_attn(self, l):
        P1, P2 = self.P1, self.P2
        H = P2[:, 0:16384].rearrange("p (k t) -> p k t", k=KT)
        Q = P1[:, 0:4096].rearrange("p (h t) -> p h t", h=4)
        O = P1[:, 4096:8192].rearrange("p (h t) -> p h t", h=4)
        M = H
        KM = P1[:, 8192:9216].rearrange("p (h t) -> p h t", h=4)
        VM = P1[:, 9216:10240].rearrange("p (m f) -> p m f", m=2)
        base = P1_OFF + 10240 * 2
        self.Pf = [self.sb([128, 256], BF16, base + i * 1024) for i in range(8)]
        self.PT = [self.sb([128, 256], BF16, base + i * 1024 + 512) for i in range(8)]
        for b in range(2):
            self.norm_pre(b * 512, 512, gcol(l, G_MEM0), lambda k: H[:, k, b * 512:(b + 1) * 512], lambda k: ("H", k, b), lambda k: ("x", k, b))
        for h in range(4):
            slot, wk = self.wload(self.d_mwk[l][h])
            b = h % 2
            for k in range(KT):
                self.op("pe", lambda e: e.matmul(self.ps(b)[:, :NMEM], lhsT=slot[:, k, :], rhs=self.memn[:, k, :], start=(k == 0), stop=(k == KT - 1)),
                        r=[wk, "memn"], w=[self.pskey(b)])
            self.op("act", lambda e: e.copy(out=KM[:, h, :], in_=self.ps(b)[:, :NMEM]), r=[self.pskey(b)], w=["KM"])
        for vt in range(2):
            slot, wk = self.wload(self.d_mwv[l][vt])
            for mt in range(2):
                b = 2 + mt
                for k in range(KT):
                    self.op("pe", lambda e: e.matmul(self.ps(b)[:, :256], lhsT=self.memn[:, k, mt * 128:(mt + 1) * 128], rhs=slot[:, k, :],
                                                     start=(k == 0), stop=(k == KT - 1)),
                            r=[wk, "memn"], w=[self.pskey(b)])
                self.op("act", lambda e: e.copy(out=VM[:, mt, vt * 256:(vt + 1) * 256], in_=self.ps(b)[:, :256]), r=[self.pskey(b)], w=["VM"])
        sc = 128.0 ** -0.5
        for h in range(4):
            slot, wk = self.wload(self.d_mwq[l][h])
            for b in range(2):
                pb = 4 + b
                for k in range(KT):
                    self.op("pe", lambda e: e.matmul(self.ps(pb), lhsT=slot[:, k, :], rhs=H[:, k, b * 512:(b + 1) * 512], start=(k == 0), stop=(k == KT - 1)),
                            r=[wk, ("H", k, b)], w=[self.pskey(pb)])
                self.op("act", lambda e: e.mul(out=Q[:, h, b * 512:(b + 1) * 512], in_=self.ps(pb), mul=sc), r=[self.pskey(pb)], w=[("Q", h, b)])
        self.barrier()
        self.dump('H', P2[:, 0:16384], [])
        self.dump('Q', P1[:, 0:4096], [])
        self.dump('KM', P1[:, 8192:9216], [])
        self.dump('VM', P1[:, 9216:10240], [])
        if CUT <= 2:
            return
        tiles = []
        for h in range(4):
            for i in range(8):
                n = len(tiles)
                tiles.append(lambda h=h, i=i, n=n: self.attn_tile(
                    n % 8, Q[:, h, i * 128:(i + 1) * 128], [("Q", h, i // 4)], KM[:, h, :], ["KM"], NMEM,
                    lambda kt: VM[:, kt, h * 128:(h + 1) * 128], ["VM"], O[:, h, i * 128:(i + 1) * 128], [("O", h, i // 4)]))
        self.attn_run(tiles, [[0], [1], [2], [3], [4], [5], [6], [7]])
        self.barrier()
        self.dump('O', P1[:, 4096:8192], [])
        if CUT <= 4:
            return
        for t in range(16):
            slot, wk = self.wload(self.d_mwo[l][t])
            for b in range(2):
                pb = (2 * t + b) % 6
                for k in range(4):
                    self.op("pe", lambda e: e.matmul(self.ps(pb), lhsT=slot[:, k, :], rhs=O[:, k, b * 512:(b + 1) * 512], start=(k == 0), stop=(k == 3)),
                            r=[wk, ("O", k, b)], w=[self.pskey(pb)])
                self.ss_acc_b(b, self.ps(pb), [self.pskey(pb)], t == 0, t == 15)
                self.op("dve", lambda e: e.tensor_copy(out=M[:, t, b * 512:(b + 1) * 512], in_=self.ps(pb)), r=[self.pskey(pb)], w=[("M", t, b), self.pskey(pb)])
        self.dump('M', P2[:, 0:16384], [])
        for b in range(2):
            self.post_apply_b(b, gcol(l, G_MEM1), lambda k: M[:, k, b * 512:(b + 1) * 512], lambda k: ("M", k, b))
        self.barrier()

    def ss_acc_b(self, b, src, keys, first, last, n=512):
        i = self.sq_i
        self.sq_i ^= 1
        sq = self.sqt[i][:, :n]
        bank = 7 - b
        self.op("act", lambda e: e.activation(out=sq, in_=src, func=AF.Square), r=keys, w=[("sqt", i)] + [k_ for k_ in keys if k_[0] == "ps"])
        self.op("pe", lambda e: e.matmul(self.ps(bank)[:, :n], lhsT=self.ones_bf, rhs=sq, start=first, stop=last),
                r=[("sqt", i), "ones"], w=[self.pskey(bank)])

    def post_apply_b(self, b, gcol_, m_fn, m_keyfn, n=512):
        bank = 7 - b
        r_ = self.rstd[:, :n]
        self.op("act", lambda e: e.activation(out=r_, in_=self.ps(bank)[:, :n], func=AF.Sqrt, bias=self.epsc, scale=1.0 / D),
                r=[self.pskey(bank), "cst2"], w=["rstd"])
        self.op("dve", lambda e: e.reciprocal(out=r_, in_=r_), r=["rstd"], w=["rstd"])
        c0 = b * 512
        for k in range(KT):
            t = self.ctmp[k % 3][:, :n]
            self.op("pool", lambda e: e.tensor_tensor(out=t, in0=m_fn(k), in1=r_, op=ALU.mult),
                    r=[m_keyfn(k), "rstd"], w=[("ctmp", k % 3)])
            xk = self.xT[:, k, c0:c0 + n]
            self.op("dve", lambda e: e.scalar_tensor_tensor(out=xk, in0=t, scalar=self.gain(gcol_, k), in1=xk,
                                                            op0=ALU.mult, op1=ALU.add),
                    r=[("ctmp", k % 3), "prm", ("x", k, b)], w=[("x", k, b)])

    def ffn(self, l):
        P1, P2 = self.P1, self.P2
        HB = P1[:, 0:8192].rearrange("p (k t) -> p k t", k=KT)
        M = P1[:, 8192:16384].rearrange("p (k t) -> p k t", k=KT)
        ACT_ = P2[:, 0:NFT * 512].rearrange("p (j t) -> p j t", j=NFT)
        hh3 = self.hh.rearrange("p (k t) -> p k t", k=KT)
        hl = self.hh0.rearrange("p (k t) -> p k t", k=KT)
        self.norm_pre(1022, 2, gcol(l, G_FFN0), lambda k: hl[:, k, :], lambda k: "hh0", lambda k: ("x", k, 1))
        self.dma("sp", self.hh_in, self.hh0, r=["hh0"], w=["hh_in"])
        self.allgather_pairs(self.hh_in, self.hh_out, r=["hh_in"], w=["hh_out"])
        self.dma("sp", self.hh0, self.hh_out[0:128, :], r=["hh_out"], w=["hh0"])
        self.op("dve", lambda e: e.tensor_scalar(out=self.hh, in0=self.hh0, scalar1=self.prm[:, ISSEC:ISSEC + 1], scalar2=None, op0=ALU.mult),
                r=["hh0", "prm"], w=["hh"])
        cw = lambda tap, tile: self.prm[:, CONV0 + l * 352 + tap * 88 + tile:CONV0 + l * 352 + tap * 88 + tile + 1]
        def up_proj(b):
                for j in range(NFT):
                    slot, wk = self.wload(self.d_wup[l][j])
                    s3 = (j % 2) * 3
                    pv, pg, ph = self.ps(s3), self.ps(s3 + 1), self.ps(s3 + 2)
                    kv_, kg_, kh_ = self.pskey(s3), self.pskey(s3 + 1), self.pskey(s3 + 2)
                    for k in range(KT):
                        self.op("pe", lambda e: e.matmul(pv, lhsT=slot[:, k, 0:128], rhs=HB[:, k, :], start=(k == 0), stop=(k == KT - 1)),
                                r=[wk, ("HB", k)], w=[kv_])
                    for k in range(KT):
                        self.op("pe", lambda e: e.matmul(pg, lhsT=slot[:, k, 128:256], rhs=HB[:, k, :], start=(k == 0), stop=(k == KT - 1)),
                                r=[wk, ("HB", k)], w=[kg_])
                    if b == 0:
                        for k in range(KT):
                            self.op("pe", lambda e: e.matmul(ph[:, 0:2], lhsT=slot[:, k, 0:128], rhs=hh3[:, k, :], start=(k == 0), stop=(k == KT - 1)),
                                    r=[wk, "hh"], w=[kh_])
                        for k in range(KT):
                            self.op("pe", lambda e: e.matmul(ph[:, 2:4], lhsT=slot[:, k, 128:256], rhs=hh3[:, k, :], start=(k == 0), stop=(k == KT - 1)),
                                    r=[wk, "hh"], w=[kh_])
                    accs = []
                    for vi, (pp, pk, tile) in enumerate(((pv, kv_, j), (pg, kg_, NFT + j))):
                        t = self.ctmp[vi]
                        tk = ("ctmp", vi)
                        self.op("act", lambda e: e.activation(out=t, in_=pp, func=AF.Identity, bias=cw(3, tile), scale=cw(2, tile)),
                                r=[pk, "prm"], w=[tk, pk])
                        self.op("dve", lambda e: e.scalar_tensor_tensor(out=t[:, 1:512], in0=pp[:, 0:511], scalar=cw(1, tile), in1=t[:, 1:512],
                                                                        op0=ALU.mult, op1=ALU.add), r=[pk, "prm", tk], w=[tk, pk])
                        self.op("dve", lambda e: e.scalar_tensor_tensor(out=t[:, 2:512], in0=pp[:, 0:510], scalar=cw(0, tile), in1=t[:, 2:512],
                                                                        op0=ALU.mult, op1=ALU.add), r=[pk, "prm", tk], w=[tk, pk])
                        if b == 0:
                            hsrc = ph[:, 2 * vi:2 * vi + 2]
                            hkeys = [kh_]
                        else:
                            hsrc = self.convhalo[:, tile, :]
                            hkeys = [("chalo", tile)]
                        self.op("dve", lambda e: e.scalar_tensor_tensor(out=t[:, 0:2], in0=hsrc, scalar=cw(0, tile), in1=t[:, 0:2],
                                                                        op0=ALU.mult, op1=ALU.add), r=hkeys + ["prm", tk], w=[tk] + [k_ for k_ in hkeys if k_[0] == "ps"])
                        self.op("dve", lambda e: e.scalar_tensor_tensor(out=t[:, 0:1], in0=hsrc[:, 1:2], scalar=cw(1, tile), in1=t[:, 0:1],
                                                                        op0=ALU.mult, op1=ALU.add), r=hkeys + ["prm", tk], w=[tk] + [k_ for k_ in hkeys if k_[0] == "ps"])
                        if b == 0:
                            self.op("act", lambda e: e.copy(out=self.convhalo[:, tile, :], in_=pp[:, 510:512]), r=[pk], w=[("chalo", tile), pk])
                        accs.append((t, tk))
                    (tv, tvk), (tg, tgk) = accs
                    sg = self.ctmp[2]
                    self.op("act", lambda e: e.activation(out=sg, in_=tg, func=AF.Silu), r=[tgk], w=[("ctmp", 2)])
                    self.op("dve", lambda e: e.tensor_tensor(out=ACT_[:, j, :], in0=tv, in1=sg, op=ALU.mult), r=[tvk, ("ctmp", 2)], w=[("ACT", j)])

        def down_proj(b):
                for i in range(16):
                    pb = i % 6
                    for half in range(2):
                        slot, wk = self.wload(self.d_wdn[l][i, half])
                        for jj in range(22):
                            j = half * 22 + jj
                            self.op("pe", lambda e: e.matmul(self.ps(pb), lhsT=slot[:, jj, :], rhs=ACT_[:, j, :], start=(j == 0), stop=(j == NFT - 1)),
                                    r=[wk, ("ACT", j)], w=[self.pskey(pb)])
                    self.ss_acc(self.ps(pb), [self.pskey(pb)], i == 0, i == 15, 512)
                    self.op("dve", lambda e: e.tensor_copy(out=M[:, i, :], in_=self.ps(pb)), r=[self.pskey(pb)], w=[("M", i), self.pskey(pb)])

        def norm_blk(b):
            self.norm_pre(b * 512, 512, gcol(l, G_FFN0), lambda k: HB[:, k, :], lambda k: ("HB", k), lambda k: ("x", k, b))

        def post_blk(b):
            self.post_apply(b * 512, 512, gcol(l, G_FFN1), lambda k: M[:, k, :], lambda k: ("M", k), lambda k: ("x", k, b))

        norm_blk(0)
        up_proj(0)
        norm_blk(1)
        down_proj(0)
        post_blk(0)
        up_proj(1)
        down_proj(1)
        post_blk(1)
        self.barrier()

    def kv_phase(self):
        P1, P2 = self.P1, self.P2
        H = P2[:, 0:16384].rearrange("p (k t) -> p k t", k=KT)
        for b in range(2):
            self.norm_pre(b * 512, 512, G_KV, lambda k: H[:, k, b * 512:(b + 1) * 512], lambda k: ("H", k, b), lambda k: ("x", k, b))
        KS = [P1[:, i * 1024:(i + 1) * 1024] for i in range(2)]
        for h in range(16):
            slot, wk = self.wload(self.d_wk[h])
            st, sk = KS[h % 2], ("KS", h % 2)
            for b in range(2):
                pb = (2 * h + b) % 6
                for k in range(KT):
                    self.op("pe", lambda e: e.matmul(self.ps(pb), lhsT=slot[:, k, :], rhs=H[:, k, b * 512:(b + 1) * 512], start=(k == 0), stop=(k == KT - 1)),
                            r=[wk, ("H", k, b)], w=[self.pskey(pb)])
                self.op("act", lambda e: e.copy(out=st[:, b * 512:(b + 1) * 512], in_=self.ps(pb)), r=[self.pskey(pb)], w=[sk])
            self.dma("sp", self.kbuf[h * 128:(h + 1) * 128, 512:1536], st, r=[sk], w=[("kbuf", h)])
            self.dma("sp", self.kx_in[h * 128:(h + 1) * 128, :], st[:, 512:1024], r=[sk], w=[("kx_in", h)])
        VS = [P1[:, 2048 + i * 256:2048 + (i + 1) * 256] for i in range(2)]
        n = 0
        for ct in range(8):
            slot, wk = self.wload(self.d_wv[ct])
            for tt in range(8):
                pb = n % 6
                st, sk = VS[n % 2], ("VS", n % 2)
                n += 1
                for k in range(KT):
                    self.op("pe", lambda e: e.matmul(self.ps(pb)[:, :256], lhsT=H[:, k, tt * 128:(tt + 1) * 128], rhs=slot[:, k, :], start=(k == 0), stop=(k == KT - 1)),
                            r=[wk, ("H", k, tt // 4)], w=[self.pskey(pb)])
                self.op("act", lambda e: e.copy(out=st, in_=self.ps(pb)[:, :256]), r=[self.pskey(pb)], w=[sk])
                self.dma("sp", self.vbuf[512 + tt * 128:512 + (tt + 1) * 128, ct * 256:(ct + 1) * 256], st, r=[sk], w=[("vbuf", n)])
                if tt >= 4:
                    self.dma("sp", self.vx_in[(tt - 4) * 128:(tt - 3) * 128, ct * 256:(ct + 1) * 256], st, r=[sk], w=[("vx_in", n)])
        self.barrier()
        self.allgather_pairs(self.kx_in, self.kx_out, w=["kx_out"])
        self.allgather_pairs(self.vx_in, self.vx_out, w=["vx_out"])
        self.dma("sp", self.kbuf[:, 0:512], self.kx_out[0:D, :], r=["kx_out"], w=["kbufh"])
        self.dma("sp", self.vbuf[0:512, :], self.vx_out[0:512, :], r=["vx_out"], w=["vbufh"])
        self.barrier()

    def mixer_b(self, l):
        P1, P2 = self.P1, self.P2
        H = P2[:, 0:16384].rearrange("p (k t) -> p k t", k=KT)
        Q = P1[:, 0:16384].rearrange("p (h t) -> p h t", h=16)
        O = H
        M = Q
        for b in range(2):
            self.norm_pre(b * 512, 512, gcol(l, G_MIX0), lambda k: H[:, k, b * 512:(b + 1) * 512], lambda k: ("H", k, b), lambda k: ("x", k, b))
        sc = 128.0 ** -0.5
        for h in range(16):
            slot, wk = self.wload(self.d_bwq[l][h])
            for b in range(2):
                pb = (2 * h + b) % 6
                for k in range(KT):
                    self.op("pe", lambda e: e.matmul(self.ps(pb), lhsT=slot[:, k, :], rhs=H[:, k, b * 512:(b + 1) * 512], start=(k == 0), stop=(k == KT - 1)),
                            r=[wk, ("H", k, b)], w=[self.pskey(pb)])
                self.op("act", lambda e: e.mul(out=Q[:, h, b * 512:(b + 1) * 512], in_=self.ps(pb), mul=sc), r=[self.pskey(pb)], w=[("Q", h, b)])
        self.barrier()
        o = T_OFF
        Kh = [self.sb([128, 1536], BF16, P2_OFF + 32768 + i * 3072) for i in range(3)]
        Vh = [self.sb([128, 12, 128], BF16, o + i * 3072) for i in range(3)]; o += 9216
        Bst = [self.sb([128, 640], F32, o)] * 3; o += 2560
        BT = [self.sb([128, 640], BF16, o + i * 1280) for i in range(3)]; o += 3840
        mT = self.sb([128, 640], F32, o); o += 2560
        self.Pf = [self.sb([128, 640], BF16, o + i * 1280) for i in range(4)]; o += 5120
        self.PT = [self.sb([128, 640], BF16, o + i * 1280) for i in range(4)]; o += 5120
        assert o <= T_OFF + 32768
        self.dma("sp", mT, self.d_maskT, w=["mT"])
        vsrc = self.vbuf.rearrange("(t p) f -> p t f", p=128)
        def load_head(h):
            s = h % 3
            self.dma("sp", Kh[s], self.kbuf[h * 128:(h + 1) * 128, :], w=[("Kh", s)])
            self.dma("sp", Vh[s], vsrc[:, :, h * 128:(h + 1) * 128], w=[("Vh", s)])
            self.dma("sp", Bst[s], self.d_bias[l][h], w=[("Bst", 0)])
            self.op("dve", lambda e: e.tensor_tensor(out=BT[s], in0=Bst[s], in1=mT, op=ALU.add), r=[("Bst", 0), "mT"], w=[("BT", s)])

        def mk(h, i, n):
            if h == 0 and i == 0:
                load_head(0)
            if i == 0 and h + 1 < 16:
                load_head(h + 1)
            s = h % 3
            return self.attn_tile(n % 4, Q[:, h, i * 128:(i + 1) * 128], [("Q", h, i // 4)], Kh[s][:, i * 128:i * 128 + 640], [("Kh", s)], 640,
                                  lambda kt: Vh[s][:, i + kt, :], [("Vh", s)], O[:, h, i * 128:(i + 1) * 128], [("O", h, i // 4)],
                                  bias=BT[s], bkeys=[("BT", s)], mask=(self.maskrow[0:1, i * 128:i * 128 + 640] if i < 4 else None))

        tiles = []
        for h in range(16):
            for i in range(8):
                n = len(tiles)
                tiles.append(lambda h=h, i=i, n=n: mk(h, i, n))
        self.attn_run(tiles, [[0], [1, 2], [3, 4], [5, 6], [7]])
        self.barrier()
        for t in range(16):
            slot, wk = self.wload(self.d_bwo[l][t])
            for b in range(2):
                pb = (2 * t + b) % 6
                for k in range(KT):
                    self.op("pe", lambda e: e.matmul(self.ps(pb), lhsT=slot[:, k, :], rhs=O[:, k, b * 512:(b + 1) * 512], start=(k == 0), stop=(k == KT - 1)),
                            r=[wk, ("O", k, b)], w=[self.pskey(pb)])
                self.ss_acc_b(b, self.ps(pb), [self.pskey(pb)], t == 0, t == 15)
                self.op("dve", lambda e: e.tensor_copy(out=M[:, t, b * 512:(b + 1) * 512], in_=self.ps(pb)), r=[self.pskey(pb)], w=[("M", t, b), self.pskey(pb)])
        for b in range(2):
            self.post_apply_b(b, gcol(l, G_MIX1), lambda k: M[:, k, b * 512:(b + 1) * 512], lambda k: ("M", k, b))
        self.barrier()

    def mixer_a(self, l):
        P1, P2 = self.P1, self.P2
        H = P2[:, 0:16384].rearrange("p (k t) -> p k t", k=KT)
        Uf = P1[:, 0:16384].rearrange("p (k t) -> p k t", k=KT)
        UG = P2[:, 0:16384].rearrange("p (g c) -> p g c", g=128)
        SX = P1[:, 0:16384].rearrange("p (m r c) -> p m r c", m=64, r=2)
        G_ = P2[:, 0:16384].rearrange("p (k t) -> p k t", k=KT)
        M = P1[:, 0:16384].rearrange("p (k t) -> p k t", k=KT)
        for b in range(2):
            self.norm_pre(b * 512, 512, gcol(l, G_MIX0), lambda k: H[:, k, b * 512:(b + 1) * 512], lambda k: ("H", k, b), lambda k: ("x", k, b))
        for t in range(16):
            slot, wk = self.wload(self.d_win[l][t])
            for b in range(2):
                pb = (2 * t + b) % 6
                for k in range(KT):
                    self.op("pe", lambda e: e.matmul(self.ps(pb), lhsT=slot[:, k, :], rhs=H[:, k, b * 512:(b + 1) * 512], start=(k == 0), stop=(k == KT - 1)),
                            r=[wk, ("H", k, b)], w=[self.pskey(pb)])
                self.op("act", lambda e: e.copy(out=Uf[:, t, b * 512:(b + 1) * 512], in_=self.ps(pb)), r=[self.pskey(pb)], w=[("P1", t)])
        self.barrier()
        SEL = self.sb([128, 4080], BF16, MISC_OFF)
        ZP = SEL[:, 0:1920].rearrange("p (g c) -> p g c", g=8)
        ZZ = SEL[:, 1920:4080]
        self.dma("pool", SEL.rearrange("p (a b) -> p a b", a=4), self.d_selc.rearrange("p (a b) -> p a b", a=4), w=["sel"])
        tabsets = []
        for j in range(2):
            d_ = {}
            for idx, nm in enumerate(("WSTre", "WSTim", "WOnre", "WOnim", "WS2re", "WS2im")):
                d_[nm] = self.sb([128, 8, 128], BF16, T_OFF + j * 12288 + idx * 2048)
            tabsets.append(d_)
        WOre = self.sb([128, 8, 128], BF16, T_OFF + 24576)
        WOim = self.sb([128, 8, 128], BF16, T_OFF + 24576 + 2048)
        WOpair = self.sb([128, 2, 8, 128], BF16, T_OFF + 24576)
        TOEP = self.sb([128, 16, 128], BF16, T_OFF + 28672)
        o = P2_OFF + 32768
        SCR0 = o
        EC = self.sb([128, 16, 8], F32, o); o += 512
        ES = self.sb([128, 16, 8], F32, o); o += 512
        BCs = [self.sb([128, 8, 16], F32, o + i * 512) for i in range(4)]; o += 2048
        BbR = self.sb([128, 8, 16], F32, o); o += 512
        BbI = self.sb([128, 8, 16], F32, o); o += 512
        tA = self.sb([128, 4, 8, 16], F32, o); o += 2048
        tB = self.sb([128, 4, 8, 16], F32, o); o += 2048
        LAM = self.sb([128, 3, 64], F32, o); o += 768
        DLT = self.sb([128, 3, 64], F32, o); o += 768
        FF = self.sb([128, 2, 64], F32, o); o += 512
        TS = [self.sb([128, 16, 8], F32, o + i * 512) for i in range(3)]; o += 1536
        ANG = self.sb([128, 16, 8], F32, o); o += 512
        assert o <= P2_OFF + 45056, o
        WSCR = o - 2048
        X = [self.xs5[0], self.xs5[1]]
        T1, T2, AR, AI2 = self.xs5[2], self.xs5[3], self.xs5_ar, self.xs5_ai2
        kv = self.kvph[:, 0:16]
        self.dma("sp", LAM, self.d_s5lam[l], w=["lam"])
        lr, li, ld = LAM[:, 0, :], LAM[:, 1, :], LAM[:, 2, :]
        dt, lrdt, th = DLT[:, 0, :], DLT[:, 1, :], DLT[:, 2, :]
        tt_ = lambda out, a, b, op, r=(), w=(): self.op("dve", lambda e: e.tensor_tensor(out=out, in0=a, in1=b, op=op), r=list(r), w=list(w))
        I32 = mybir.dt.int32

        def sin_(out, arg, phase, r, w):
            shp = list(arg.shape)
            if len(shp) == 2:
                fs, is_ = self.s5f[:, 0:shp[1]], self.s5i[:, 0:shp[1]]
            else:
                fs = self.s5f.rearrange("p (a b) -> p a b", a=shp[1])
                is_ = self.s5i.rearrange("p (a b) -> p a b", a=shp[1])
            k_ = ["sarg"]
            self.op("dve", lambda e: e.tensor_scalar(out=arg, in0=arg, scalar1=float(phase), scalar2=None, op0=ALU.add), r=list(r), w=k_)
            self.op("dve", lambda e: e.tensor_scalar(out=fs, in0=arg, scalar1=1.0 / (2 * PI), scalar2=None, op0=ALU.mult), r=k_, w=["sfs"])
            self.op("dve", lambda e: e.tensor_copy(out=is_, in_=fs), r=["sfs"], w=["sis"])
            self.op("dve", lambda e: e.tensor_copy(out=fs, in_=is_), r=["sis"], w=["sfs"])
            self.op("dve", lambda e: e.scalar_tensor_tensor(out=arg, in0=fs, scalar=-2 * PI, in1=arg, op0=ALU.mult, op1=ALU.add), r=["sfs"] + k_, w=k_)
            self.op("dve", lambda e: e.tensor_scalar(out=fs, in0=arg, scalar1=PI, scalar2=-2 * PI, op0=ALU.is_gt, op1=ALU.mult), r=k_, w=["sfs"])
            self.op("dve", lambda e: e.tensor_tensor(out=arg, in0=arg, in1=fs, op=ALU.add), r=["sfs"] + k_, w=k_)
            self.op("dve", lambda e: e.tensor_scalar(out=fs, in0=arg, scalar1=-PI, scalar2=2 * PI, op0=ALU.is_lt, op1=ALU.mult), r=k_, w=["sfs"])
            self.op("dve", lambda e: e.tensor_tensor(out=arg, in0=arg, in1=fs, op=ALU.add), r=["sfs"] + k_, w=k_)
            self.op("act", lambda e: e.activation(out=out, in_=arg, func=AF.Sin), r=k_, w=list(w))

        self.op("act", lambda e: e.activation(out=dt, in_=ld, func=AF.Exp), r=["lam"], w=["dlt"])
        tt_(lrdt, lr, dt, ALU.mult, ["lam", "dlt"], ["dlt"])
        tt_(th, li, dt, ALU.mult, ["lam", "dlt"], ["dlt"])
        W = [self.sb([128, 64], F32, WSCR + i * 256) for i in range(8)]
        c1, s1, mag, abr, abi, nr, den, w7 = W
        self.op("dve", lambda e: e.tensor_copy(out=c1, in_=th), r=["dlt"], w=["c1"])
        sin_(c1, c1, PI / 2, ["c1"], ["c1"])
        self.op("dve", lambda e: e.tensor_copy(out=s1, in_=th), r=["dlt"], w=["s1"])
        sin_(s1, s1, 0.0, ["s1"], ["s1"])
        self.op("act", lambda e: e.activation(out=mag, in_=lrdt, func=AF.Exp), r=["dlt"], w=["mag"])
        tt_(abr, mag, c1, ALU.mult, ["mag", "c1"], ["abr"])
        tt_(abi, mag, s1, ALU.mult, ["mag", "s1"], ["abi"])
        self.op("dve", lambda e: e.tensor_scalar(out=nr, in0=abr, scalar1=-1.0, scalar2=None, op0=ALU.add), r=["abr"], w=["nr"])
        tt_(den, lr, lr, ALU.mult, ["lam"], ["den"])
        tt_(w7, li, li, ALU.mult, ["lam"], ["w7"])
        tt_(den, den, w7, ALU.add, ["den", "w7"], ["den"])
        self.op("dve", lambda e: e.reciprocal(out=den, in_=den), r=["den"], w=["den"])
        fre, fim = FF[:, 0, :], FF[:, 1, :]
        tt_(fre, nr, lr, ALU.mult, ["nr", "lam"], ["fre"])
        tt_(w7, abi, li, ALU.mult, ["abi", "lam"], ["w7"])
        tt_(fre, fre, w7, ALU.add, ["fre", "w7"], ["fre"])
        tt_(fre, fre, den, ALU.mult, ["fre", "den"], ["fre"])
        tt_(fim, abi, lr, ALU.mult, ["abi", "lam"], ["fim"])
        tt_(w7, nr, li, ALU.mult, ["nr", "lam"], ["w7"])
        tt_(fim, fim, w7, ALU.subtract, ["fim", "w7"], ["fim"])
        tt_(fim, fim, den, ALU.mult, ["fim", "den"], ["fim"])
        self.barrier()
        if CUT == 51:
            return
        import functools

        def gen(gb):
            E = []
            add = lambda fn, *a: E.append(functools.partial(fn, *a))
            tabs = tabsets[gb % 2]
            st = gb % 2
            m0 = 8 * gb
            for i in range(4):
                add(lambda i=i: self.dma("sp", BCs[i], self.d_s5bc[l][i][:, m0:m0 + 8, :], w=[("bc", i)]))
            BR, BI, CR, CI = BCs
            kvb = bc(kv.unsqueeze(2), [128, 16, 8])
            thb = bc(th[:, m0:m0 + 8].unsqueeze(1), [128, 16, 8])
            lrb = bc(lrdt[:, m0:m0 + 8].unsqueeze(1), [128, 16, 8])
            Ct, St, Et = TS
            add(tt_, ANG, kvb, thb, ALU.mult, ["dlt"], ["ang"])
            add(lambda: self.op("dve", lambda e: e.tensor_copy(out=Ct, in_=ANG), r=["ang"], w=["Ct"]))
            add(sin_, Ct, Ct, PI / 2, ["Ct"], ["Ct"])
            add(sin_, ANG, ANG, 0.0, ["ang"], ["ang"])
            add(tt_, Et, kvb, lrb, ALU.mult, ["dlt"], ["Et"])
            add(lambda: self.op("act", lambda e: e.activation(out=Et, in_=Et, func=AF.Exp), r=["Et"], w=["Et"]))
            add(tt_, EC, Et, Ct, ALU.mult, ["Et", "Ct"], ["EC"])
            add(tt_, ES, Et, ANG, ALU.mult, ["Et", "ang"], ["ES"])
            add(lambda: self.op("act", lambda e: e.copy(out=AR[:, m0:m0 + 8], in_=EC[:, 15, :]), r=["EC"], w=["AR"]))
            add(lambda: self.op("act", lambda e: e.copy(out=AI2[:, m0:m0 + 8, 1], in_=ES[:, 15, :]), r=["ES"], w=["AI2"]))
            add(lambda: self.op("act", lambda e: e.mul(out=AI2[:, m0:m0 + 8, 0], in_=ES[:, 15, :], mul=-1.0), r=["ES"], w=["AI2"]))
            freb = bc(fre[:, m0:m0 + 8].unsqueeze(2), [128, 8, 16])
            fimb = bc(fim[:, m0:m0 + 8].unsqueeze(2), [128, 8, 16])
            a3 = tA[:, 0:1, :, :].rearrange("p a s h -> p (a s) h")
            b3 = tB[:, 0:1, :, :].rearrange("p a s h -> p (a s) h")
            add(tt_, a3, freb, BR, ALU.mult, [("bc", 0)], ["tA"])
            add(tt_, b3, fimb, BI, ALU.mult, [("bc", 1)], ["tB"])
            add(tt_, BbR, a3, b3, ALU.subtract, ["tA", "tB"], ["BbR"])
            add(tt_, a3, freb, BI, ALU.mult, [("bc", 1)], ["tA"])
            add(tt_, b3, fimb, BR, ALU.mult, [("bc", 0)], ["tB"])
            add(tt_, BbI, a3, b3, ALU.add, ["tA", "tB"], ["BbI"])
            rk = ["EC", "ES", "BbR", "BbI", ("bc", 2), ("bc", 3)]

            def table(dst, key, e1, x1, e2, x2, comb):
                add(tt_, tA, e1, x1, ALU.mult, rk, ["tA"])
                add(tt_, tB, e2, x2, ALU.mult, rk, ["tB"])
                if comb == "sub":
                    add(tt_, dst, tA, tB, ALU.subtract, ["tA", "tB"], [key])
                elif comb == "add":
                    add(tt_, dst, tA, tB, ALU.add, ["tA", "tB"], [key])
                else:
                    add(lambda: self.op("dve", lambda e: e.scalar_tensor_tensor(out=dst, in0=tA, scalar=-1.0, in1=tB, op0=ALU.mult, op1=ALU.subtract),
                                        r=["tA", "tB"], w=[key]))

            for hb in range(2):
                mh = 4 * hb

                def kview(tab, a_, b_, step, mh=mh):
                    sl = tab[:, a_:b_:step, mh:mh + 4]
                    return bc(sl.rearrange("p s m -> p m s").unsqueeze(3), [128, 4, 8, 16])

                def bview(t3, mh=mh):
                    return bc(t3[:, mh:mh + 4, :].unsqueeze(2), [128, 4, 8, 16])

                def ov(t, mh=mh):
                    return t[:, mh:mh + 4, :].rearrange("p m (s h) -> p m s h", s=8)

                ecS, esS = kview(EC, 14, 6, -1), kview(ES, 14, 6, -1)
                table(ov(tabs["WSTre"]), ("tab", "WSTre", st), ecS, bview(BbR), esS, bview(BbI), "sub")
                table(ov(tabs["WSTim"]), ("tab", "WSTim", st), esS, bview(BbR), ecS, bview(BbI), "add")
                ecN, esN = kview(EC, 0, 8, 1), kview(ES, 0, 8, 1)
                table(ov(tabs["WOnre"]), ("tab", "WOnre", st), ecN, bview(CR), esN, bview(CI), "sub")
                table(ov(tabs["WOnim"]), ("tab", "WOnim", st), esN, bview(CR), ecN, bview(CI), "negadd")
                ecP, esP = kview(EC, 8, 16, 1), kview(ES, 8, 16, 1)
                table(ov(WOre), ("tab", "WOre"), ecP, bview(CR), esP, bview(CI), "sub")
                table(ov(WOim), ("tab", "WOim"), esP, bview(CR), ecP, bview(CI), "negadd")
            return E

        for fn in gen(0):
            fn()
        for gb in range(8):
            m0 = 8 * gb
            st = gb % 2
            tabs = tabsets[st]
            self.dma("sp", self.tb_wo[gb], WOpair, r=[("tab", "WOre"), ("tab", "WOim")], w=[("tbwo", gb)])
            nxt = gen(gb + 1) if gb < 7 else []
            nsl = 24
            per = (len(nxt) + nsl - 1) // nsl
            pos = [0]

            def pump():
                for fn in nxt[pos[0]:pos[0] + per]:
                    fn()
                pos[0] += per

            tabkeys = [("tab", nm, st) for nm in ("WSTre", "WSTim", "WOnre", "WOnim")]
            for ml in range(8):
                pump()
                for ri, nm in enumerate(("WSTre", "WSTim")):
                    pb = ri
                    tp = self.ps(pb).bitcast(BF16)[:, 0:128]
                    self.op("pe", lambda e: e.transpose(out=tp, in_=tabs[nm][:, ml, :], identity=self.ident_bf), r=[("tab", nm, st), "ident"], w=[self.pskey(pb)])
                    dst = tabs["WS2re" if ri == 0 else "WS2im"][:, ml, :]
                    self.op("act", lambda e: e.copy(out=dst, in_=tp), r=[self.pskey(pb)], w=[("ws2", ri, st)])
                for par in range(2):
                    gl = 2 * ml + par
                    pb = 2 + par
                    lo, hi = 64 * par, 64 * par + 64
                    self.op("pe", lambda e: e.matmul(self.ps(pb)[:, 0:128], lhsT=tabs["WSTre"][lo:hi, ml, :], rhs=tabs["WOnre"][lo:hi, ml, :], start=True, stop=False),
                            r=tabkeys, w=[self.pskey(pb)])
                    self.op("pe", lambda e: e.matmul(self.ps(pb)[:, 0:128], lhsT=tabs["WSTim"][lo:hi, ml, :], rhs=tabs["WOnim"][lo:hi, ml, :], start=False, stop=True),
                            r=tabkeys, w=[self.pskey(pb)])
                    tg = TOEP[:, gl, :]
                    self.op("dve", lambda e: e.tensor_tensor(out=tg, in0=self.ps(pb)[:, 0:128], in1=self.cmask, op=ALU.mult), r=[self.pskey(pb), "cst"], w=["toep"])
                    gcolumn = self.prm[:, DD0 + l * 128 + 16 * gb + gl:DD0 + l * 128 + 16 * gb + gl + 1]
                    self.op("dve", lambda e: e.scalar_tensor_tensor(out=tg, in0=self.ident_bf, scalar=gcolumn, in1=tg, op0=ALU.mult, op1=ALU.add),
                            r=["toep", "prm", "ident"], w=["toep"])
            self.dma("sp", self.tb_toep[gb], TOEP, r=["toep"], w=[("tbtoep", gb)])
            for T in (2 * gb, 2 * gb + 1):
                for gl in range(8):
                    pump()
                    g = 8 * T + gl
                    pb = 4 + (g % 2)
                    for j in range(8):
                        self.op("pe", lambda e: e.matmul(self.ps(pb)[:, 0:128], lhsT=ZP[:, gl, 112 - 16 * j:240 - 16 * j], rhs=Uf[:, T, j:1024:8], start=(j == 0), stop=(j == 7)),
                                r=["sel", ("P1", T)], w=[self.pskey(pb)])
                    self.op("act", lambda e: e.copy(out=UG[:, g, :], in_=self.ps(pb)[:, 0:128]), r=[self.pskey(pb)], w=[("UG", g)])
                for mq in range(4):
                    m = 4 * T + mq
                    ml = m - m0
                    for ri in range(2):
                        pb = 6 + ri
                        ws2 = tabs["WS2re" if ri == 0 else "WS2im"]
                        for par in range(2):
                            lo, hi = 64 * par, 64 * par + 64
                            self.op("pe", lambda e: e.matmul(self.ps(pb)[lo:hi, 0:128], lhsT=ws2[:, ml, lo:hi], rhs=UG[:, 2 * m + par, :], start=True, stop=True),
                                    r=[("ws2", ri, st), ("UG", 2 * m + par)], w=[self.pskey(pb)])
                        self.op("act", lambda e: e.copy(out=SX[:, m, ri, :], in_=self.ps(pb)[:, 0:128]), r=[self.pskey(pb)], w=[("P1", T), ("SX", m)])
            while pos[0] < len(nxt):
                pump()
        self.barrier()
        if CUT == 52:
            return
        self.dump("UG", P2[:, 0:16384], [])
        self.dump("S", P1[:, 0:16384], [])

        NSEG, SL = 4, 32
        o = T_OFF
        XS = [self.sb([128, NSEG, 64, 2], F32, o + i * 2048) for i in range(2)]; o += 4096
        W1 = self.sb([128, NSEG, 64, 2], F32, o); o += 2048
        W2 = self.sb([128, NSEG, 64, 2], F32, o); o += 2048
        MR = self.sb([128, 64], F32, o); o += 256
        MI = self.sb([128, 64], F32, o); o += 256
        MI2 = self.sb([128, 64, 2], F32, o); o += 512
        q1 = self.sb([128, 64], F32, o); o += 256
        q2 = self.sb([128, 64], F32, o); o += 256
        q3 = self.sb([128, 64], F32, o); o += 256
        ACC = self.sb([128, 64, 2], F32, o); o += 512
        c1_, c2_ = self.sb([128, 64, 2], F32, o), self.sb([128, 64, 2], F32, o + 512); o += 1024
        SXs = SX.rearrange("p m r (h j) -> p h m r j", h=NSEG)
        ARb4 = bc(AR.unsqueeze(1).unsqueeze(3), [128, NSEG, 64, 2])
        AI4 = bc(AI2.unsqueeze(1), [128, NSEG, 64, 2])

        def scan(store):
            for j in range(SL):
                xc, xn = XS[j % 2], XS[(j + 1) % 2]
                sc_ = SXs[:, :, :, :, j]
                tt_(W1, xc, ARb4, ALU.mult, [("X", j % 2), "AR"], ["W1"])
                tt_(W2, xc[:, :, :, ::-1], AI4, ALU.mult, [("X", j % 2), "AI2"], ["W2"])
                tt_(W1, W1, W2, ALU.add, ["W1", "W2"], ["W1"])
                tt_(xn, W1, sc_, ALU.add, ["W1", ("Sc", j)], [("X", (j + 1) % 2)])
                if store:
                    self.op("act", lambda e: e.copy(out=sc_, in_=xc), r=[("X", j % 2)], w=[("Sc", j)])

        def cmul_add(dst, src, add, mr_b, mi2, keys_r, key_w):
            tt_(c1_, src, mr_b, ALU.mult, keys_r, ["c1_"])
            tt_(c2_, src[:, :, ::-1], mi2, ALU.mult, keys_r, ["c2_"])
            tt_(c1_, c1_, c2_, ALU.add, ["c1_", "c2_"], ["c1_"])
            tt_(dst, c1_, add, ALU.add, ["c1_"] + keys_r, key_w)

        self.op("dve", lambda e: e.tensor_copy(out=MR, in_=AR), r=["AR"], w=["MR"])
        self.op("dve", lambda e: e.tensor_copy(out=MI, in_=AI2[:, :, 1]), r=["AI2"], w=["MI"])
        for _ in range(5):
            tt_(q1, MR, MR, ALU.mult, ["MR"], ["q1"])
            tt_(q2, MI, MI, ALU.mult, ["MI"], ["q2"])
            tt_(q3, MR, MI, ALU.mult, ["MR", "MI"], ["q3"])
            tt_(MR, q1, q2, ALU.subtract, ["q1", "q2"], ["MR"])
            self.op("dve", lambda e: e.tensor_scalar(out=MI, in0=q3, scalar1=2.0, scalar2=None, op0=ALU.mult), r=["q3"], w=["MI"])
        self.op("dve", lambda e: e.tensor_copy(out=MI2[:, :, 1], in_=MI), r=["MI"], w=["MI2"])
        self.op("dve", lambda e: e.tensor_scalar(out=MI2[:, :, 0], in0=MI, scalar1=-1.0, scalar2=None, op0=ALU.mult), r=["MI"], w=["MI2"])
        MRb = bc(MR.unsqueeze(2), [128, 64, 2])
        mk_ = ["MR", "MI2"]
        self.op("dve", lambda e: e.memset(XS[0], 0.0), w=[("X", 0)])
        scan(False)
        E = XS[SL % 2]
        ek = ("X", SL % 2)
        self.op("dve", lambda e: e.tensor_copy(out=ACC, in_=E[:, 0]), r=[ek], w=["ACC"])
        for h in range(1, NSEG):
            cmul_add(ACC, ACC, E[:, h], MRb, MI2, ["ACC", ek] + mk_, ["ACC"])
        accf = ACC.rearrange("p m r -> p (m r)")
        self.dma("sp", self.xs_in, accf, r=["ACC"], w=["xs_in"])
        self.allgather_pairs(self.xs_in, self.xs_out, r=["xs_in"], w=["xs_out"])
        self.dma("sp", accf, self.xs_out[0:128, :], r=["xs_out"], w=["ACC"])
        self.op("dve", lambda e: e.tensor_scalar(out=accf, in0=accf, scalar1=self.prm[:, ISSEC:ISSEC + 1], scalar2=None, op0=ALU.mult),
                r=["ACC", "prm"], w=["ACC"])
        Xst = XS[(SL + 1) % 2]
        sk_ = ("X", (SL + 1) % 2)
        self.op("dve", lambda e: e.tensor_copy(out=Xst[:, 0], in_=ACC), r=["ACC"], w=[sk_])
        for h in range(1, NSEG):
            cmul_add(Xst[:, h], Xst[:, h - 1], E[:, h - 1], MRb, MI2, [sk_, ek] + mk_, [sk_])
        if (SL + 1) % 2 != 0:
            self.op("dve", lambda e: e.tensor_copy(out=XS[0], in_=Xst), r=[sk_], w=[("X", 0)])
        scan(True)
        self.barrier()
        if CUT == 53:
            return
        self.dump("XP", P1[:, 0:16384], [])

        TB = [self.sb([128, 16, 128], BF16, T_OFF + i * 8192) for i in range(2)]
        WB = [self.sb([128, 2, 8, 128], BF16, T_OFF + i * 8192 + 4096) for i in range(2)]
        YG = [self.sb([128, 8, 128], BF16, T_OFF + 16384 + i * 2048) for i in range(2)]
        GT = self.sb([128, 1024], F32, T_OFF + 20480)
        nt = 0
        for gb in range(8):
            s = gb % 2
            self.dma("sp", TB[s], self.tb_toep[gb], w=[("TB", s)])
            self.dma("sp", WB[s], self.tb_wo[gb], w=[("WB", s)])
            for T in (2 * gb, 2 * gb + 1):
                yg, ygk = YG[nt % 2], ("YG", nt % 2)
                for gl in range(8):
                    g = 8 * T + gl
                    m, par = g // 2, g % 2
                    lo, hi = 64 * par, 64 * par + 64
                    pb = (0, 1, 6, 7)[g % 4]
                    self.op("pe", lambda e: e.matmul(self.ps(pb)[:, 0:128], lhsT=TB[s][:, g - 16 * gb, :], rhs=UG[:, g, :], start=True, stop=False),
                            r=[("TB", s), ("P2", T)], w=[self.pskey(pb)])
                    self.op("pe", lambda e: e.matmul(self.ps(pb)[:, 0:128], lhsT=WB[s][lo:hi, 0, m - 8 * gb, :], rhs=SX[lo:hi, m, 0, :], start=False, stop=False),
                            r=[("WB", s)], w=[self.pskey(pb)])
                    self.op("pe", lambda e: e.matmul(self.ps(pb)[:, 0:128], lhsT=WB[s][lo:hi, 1, m - 8 * gb, :], rhs=SX[lo:hi, m, 1, :], start=False, stop=True),
                            r=[("WB", s)], w=[self.pskey(pb)])
                    self.op("act", lambda e: e.copy(out=yg[:, gl, :], in_=self.ps(pb)[:, 0:128]), r=[self.pskey(pb)], w=[ygk])
                po = self.ps2[1 + nt % 2]
                pk = [self.pskey(2 + 2 * (nt % 2)), self.pskey(3 + 2 * (nt % 2))]
                for t in range(8):
                    for gl in range(8):
                        c0 = 112 + 256 * t - 16 * gl
                        self.op("pe", lambda e: e.matmul(po[:, t * 128:(t + 1) * 128], lhsT=ZZ[:, c0:c0 + 128], rhs=yg[:, gl, :], start=(gl == 0), stop=(gl == 7)),
                                r=["sel", ygk], w=pk)
                for hf in range(2):
                    pkh = [pk[hf]]
                    ph = po[:, hf * 512:(hf + 1) * 512]
                    gt = GT[:, hf * 512:(hf + 1) * 512]
                    gk = ("GT", hf)
                    self.op("act", lambda e: e.activation(out=gt, in_=ph, func=AF.Square), r=pkh, w=[gk] + pkh)
                    self.op("dve", lambda e: e.tensor_scalar(out=gt, in0=gt, scalar1=0.044715, scalar2=1.0, op0=ALU.mult, op1=ALU.add), r=[gk], w=[gk])
                    self.op("dve", lambda e: e.tensor_tensor(out=gt, in0=gt, in1=ph, op=ALU.mult), r=[gk] + pkh, w=[gk] + pkh)
                    self.op("act", lambda e: e.activation(out=gt, in_=gt, func=AF.Sigmoid, scale=1.5957691216057308), r=[gk], w=[gk])
                    gout = G_[:, T, :].rearrange("p (c t) -> p t c", t=8)[:, 4 * hf:4 * hf + 4, :]
                    self.op("dve", lambda e: e.tensor_tensor(out=gout, in0=gt.rearrange("p (t c) -> p t c", t=4), in1=ph.rearrange("p (t c) -> p t c", t=4), op=ALU.mult),
                            r=[gk] + pkh, w=[("P2", T)] + pkh)
                nt += 1
        self.barrier()
        if CUT == 54:
            return
        self.dump("G", P2[:, 0:16384], [])
        for i in range(16):
            slot, wk = self.wload(self.d_wglu[l][i])
            for b in range(2):
                pv, pg = (4 * i + 2 * b) % 6, (4 * i + 2 * b) % 6 + 1
                for k in range(KT):
                    self.op("pe", lambda e: e.matmul(self.ps(pv), lhsT=slot[:, k, 0:128], rhs=G_[:, k, b * 512:(b + 1) * 512], start=(k == 0), stop=(k == KT - 1)),
                            r=[wk], w=[self.pskey(pv)])
                for k in range(KT):
                    self.op("pe", lambda e: e.matmul(self.ps(pg), lhsT=slot[:, k, 128:256], rhs=G_[:, k, b * 512:(b + 1) * 512], start=(k == 0), stop=(k == KT - 1)),
                            r=[wk], w=[self.pskey(pg)])
                sg, mf = self.ctmp[0], self.ctmp[1]
                self.op("act", lambda e: e.activation(out=sg, in_=self.ps(pg), func=AF.Sigmoid), r=[self.pskey(pg)], w=[("ctmp", 0)])
                self.op("dve", lambda e: e.tensor_tensor(out=mf, in0=self.ps(pv), in1=sg, op=ALU.mult), r=[self.pskey(pv), ("ctmp", 0)], w=[("ctmp", 1)])
                self.ss_acc_b(b, mf, [("ctmp", 1)], i == 0, i == 15)
                self.op("dve", lambda e: e.tensor_copy(out=M[:, i, b * 512:(b + 1) * 512], in_=mf), r=[("ctmp", 1)], w=[("M", i, b)])
        for b in range(2):
            self.post_apply_b(b, gcol(l, G_MIX1), lambda k: M[:, k, b * 512:(b + 1) * 512], lambda k: ("M", k, b))
        self.barrier()

    def store_out(self):
        for k in range(KT):
            self.dma("sp", self.d_out[k * 128:(k + 1) * 128, :], self.xT[:, k, :], r=[("x", k, 0), ("x", k, 1)], w=[("out", k)])

    def build(self):
        self.setup()
        for st in self.stages:
            if st == "memprep":
                self.mem_prepare()
            elif st[0] == "mem":
                self.mem_attn(st[1])
            elif st[0] == "ffn":
                self.ffn(st[1])
            elif st[0] == "mixa":
                self.mixer_a(st[1])
            elif st[0] == "kv":
                self.kv_phase()
            elif st[0] == "mixb":
                self.mixer_b(st[1])
        self.store_out()
        self.barrier()


def _tile_w(W, cw, nk=None):
    din, dout = W.shape
    return np.ascontiguousarray(W.reshape(din // 128, 128, dout // cw, cw).transpose(2, 1, 0, 3))


def _gain16(v):
    return v.reshape(16, 128).T


_CACHE = {}


def _consts():
    if "c" in _CACHE:
        return _CACHE["c"]
    cst = np.zeros((128, 288), np.float32)
    cst[:, 0:128] = np.eye(128, dtype=np.float32)
    r = np.arange(128)
    cst[:, 128:256] = ((r[None, :] // 16) >= (r[:, None] // 16)).astype(np.float32)
    cst[:, 256:272] = np.arange(-7, 9, dtype=np.float32)[None, :]
    cst[:, 272] = PI / 2 + 65 * PI
    cst[:, 273] = 65 * PI
    q = np.arange(128)[:, None]
    c = np.arange(640)[None, :]
    ok = np.where(q < 64, c < 576, c >= 64)
    maskT = np.where(ok, 0.0, -1e30).astype(np.float32)
    idx = np.clip(q - c + 512, -63, 256) + 63
    zpad = np.zeros((128, 8, 240), np.float32)
    for gl in range(8):
        for h in range(16):
            zpad[16 * gl + h, gl, 112 + h] = 1.0
    zz = np.zeros((128, 2160), np.float32)
    for t in range(8):
        for h in range(16):
            zz[16 * t + h, 112 + 256 * t + h] = 1.0
    selc = np.concatenate([zpad.reshape(128, 1920), zz], axis=1)
    _CACHE["c"] = (cst, maskT, idx, selc)
    return _CACHE["c"]


def prep_inputs(inp, names):
    f = lambda a: np.ascontiguousarray(np.asarray(a, dtype=np.float32))
    cst, maskT, idx, selc = _consts()
    names = set(names)
    sh = {}
    sh["cst"] = cst
    sh["maskT"] = maskT
    sh["selc"] = selc
    prm = np.zeros((128, NPRM), np.float32)
    for l in range(DEPTH):
        for w_, name in ((G_MIX0, "norm_mix"), (G_MEM0, "norm_mem"), (G_FFN0, "norm_ffn")):
            for s in range(2):
                c0 = gcol(l, w_ + s)
                prm[:, c0:c0 + 16] = _gain16(f(inp[name])[l, s])
        cwv = f(inp["f_conv_w"])[l]
        for tap in range(3):
            prm[:, CONV0 + l * 352 + tap * 88:CONV0 + l * 352 + tap * 88 + 88] = cwv[tap].reshape(88, 128).T
        prm[:, CONV0 + l * 352 + 3 * 88:CONV0 + l * 352 + 4 * 88] = f(inp["f_conv_b"])[l].reshape(88, 128).T
    prm[:, G_MEMIN:G_MEMIN + 16] = _gain16(f(inp["mem_in_norm"]))
    prm[:, G_KV:G_KV + 16] = _gain16(f(inp["kv_norm"]))
    for l in range(NA):
        d = f(inp["a_d"])[l].reshape(128, 16)
        prm[:, DD0 + l * 128:DD0 + (l + 1) * 128] = np.tile(d.T, (8, 1))
    for l in range(NA):
        if "s5lam_%d" % l in names:
            s5lam = np.zeros((128, 3, 64), np.float32)
            s5bc = np.zeros((4, 128, 64, 16), np.float32)
            lr = f(inp["a_lam_re"])[l].reshape(64, 2, 64)
            li = f(inp["a_lam_im"])[l].reshape(64, 2, 64)
            ld = f(inp["a_log_dt"])[l].reshape(64, 2)
            s5lam[:, 0, :] = lr.transpose(1, 2, 0).reshape(128, 64)
            s5lam[:, 1, :] = li.transpose(1, 2, 0).reshape(128, 64)
            s5lam[:, 2, :] = np.broadcast_to(ld.T[:, None, :], (2, 64, 64)).reshape(128, 64)
            br = f(inp["a_b_re"])[l].reshape(64, 2, 64, 16)
            bi = f(inp["a_b_im"])[l].reshape(64, 2, 64, 16)
            cr = f(inp["a_c_re"])[l].reshape(64, 2, 16, 64)
            ci = f(inp["a_c_im"])[l].reshape(64, 2, 16, 64)
            s5bc[0] = br.transpose(1, 2, 0, 3).reshape(128, 64, 16)
            s5bc[1] = bi.transpose(1, 2, 0, 3).reshape(128, 64, 16)
            s5bc[2] = cr.transpose(1, 3, 0, 2).reshape(128, 64, 16)
            s5bc[3] = ci.transpose(1, 3, 0, 2).reshape(128, 64, 16)
            sh["s5lam_%d" % l] = s5lam
            sh["s5bc_%d" % l] = s5bc
            sh["a_w_in_%d" % l] = _tile_w(f(inp["a_w_in"][l]), 128)
            wg = f(inp["a_w_glu"][l])
            sh["a_w_glu_%d" % l] = np.concatenate([_tile_w(wg[:, :D], 128), _tile_w(wg[:, D:], 128)], axis=3)
    if "w_k" in names:
        sh["w_k"] = _tile_w(f(inp["w_k"]), 128)
        sh["w_v"] = _tile_w(f(inp["w_v"]), 256)
    for l in range(NA, DEPTH):
        if "b_w_q_%d" % l in names:
            j = l - NA
            sh["b_w_q_%d" % l] = _tile_w(f(inp["b_w_q"][j]), 128)
            sh["b_w_o_%d" % l] = _tile_w(f(inp["b_w_o"][j]), 128)
            sh["b_bias_%d" % l] = np.ascontiguousarray(f(inp["b_rel_bias"][j])[:, idx])
    for l in range(DEPTH):
        if "m_w_q_%d" % l in names:
            sh["m_w_q_%d" % l] = _tile_w(f(inp["m_w_q"][l]), 128)
            mkv = f(inp["m_w_kv"][l])
            sh["m_w_k_%d" % l] = _tile_w(mkv[:, :512], 128)
            sh["m_w_v_%d" % l] = _tile_w(mkv[:, 512:], 256)
            sh["m_w_o_%d" % l] = _tile_w(f(inp["m_w_o"][l]), 128)
        if "f_w_up_%d" % l in names:
            wu = f(inp["f_w_up"][l])
            sh["f_w_up_%d" % l] = np.concatenate([_tile_w(wu[:, :DFF], 128), _tile_w(wu[:, DFF:], 128)], axis=3)
            wd = f(inp["f_w_down"][l])
            sh["f_w_down_%d" % l] = np.ascontiguousarray(wd.reshape(2, 22, 128, 16, 128).transpose(3, 0, 2, 1, 4))
    x = f(inp["x"])
    mem = f(inp["mem"])
    maps = []
    for c in range(8):
        b, half = c // 2, c % 2
        m = dict(sh)
        m["xT"] = np.ascontiguousarray(x[b, half * NTOK:(half + 1) * NTOK, :].T)
        m["memT"] = np.ascontiguousarray(mem[b].T)
        p = prm.copy()
        p[:, ISSEC] = float(half)
        m["prm"] = p
        hm = np.zeros((1, 1152), np.float32)
        if half == 0:
            hm[0, :512] = -1e30
        m["hmask"] = hm
        maps.append({k: v for k, v in m.items() if k in names})
    return maps


FULL_STAGES = ["memprep"]
for _l in range(DEPTH):
    if _l == NA:
        FULL_STAGES.append(("kv",))
    FULL_STAGES.append(("mixa", _l) if _l < NA else ("mixb", _l))
    FULL_STAGES.append(("mem", _l))
    FULL_STAGES.append(("ffn", _l))


def run(inputs, stages, dumps=(), trace=False):
    prog = Prog(stages, dumps)
    maps = prep_inputs(inputs, prog.in_names)
    res = run_bass_kernel_spmd(prog.nc, maps, core_ids=list(range(8)), trace=trace)
    out = np.zeros((4, 2 * NTOK, D), np.float32)
    for c in range(8):
        b, half = c // 2, c % 2
        out[b, half * NTOK:(half + 1) * NTOK, :] = res.results[c]["outT"].T
    return out, res


def kernel(**inputs):
    out, _ = run(inputs, FULL_STAGES)
    return out
```

```python
import math
import os
CUT = int(os.environ.get('KCUT', '99'))
import numpy as np
import concourse.bass as bass
import concourse.mybir as mybir
from concourse.bass_utils import run_bass_kernel_spmd

F32 = mybir.dt.float32
BF16 = mybir.dt.bfloat16
AF = mybir.ActivationFunctionType
ALU = mybir.AluOpType
AX = mybir.AxisListType

D = 2048
NTOK = 1024
KT = 16
DEPTH = 4
NA = 2
DFF = 5632
NFT = 44
EPS = 1e-6
NMEM = 256
PI = math.pi

G_MIX0, G_MIX1, G_MEM0, G_MEM1, G_FFN0, G_FFN1 = range(6)


def gcol(l, which):
    return (l * 6 + which) * 16


G_MEMIN = 24 * 16
G_KV = 25 * 16
CONV0 = 26 * 16
DD0 = CONV0 + 4 * 352
ISSEC = DD0 + 256
NPRM = ISSEC + 8

SB_BASE = 16640
XT_OFF = SB_BASE
P1_OFF = SB_BASE + 65536
P2_OFF = SB_BASE + 98304
T_OFF = SB_BASE + 143360
PRM_OFF = SB_BASE + 176128
MEMN_OFF = PRM_OFF + 8384
MISC_OFF = MEMN_OFF + 8192


class KB:
    NDMA = 24

    def __init__(self):
        nc = bass.Bass("TRN2", target_bir_lowering=False)
        self.nc = nc
        self.eng = {"pe": nc.tensor, "act": nc.scalar, "dve": nc.vector, "pool": nc.gpsimd, "sp": nc.sync}
        self.prog = {e: nc.alloc_semaphore("prog_" + e) for e in self.eng}
        self.cnt = {e: 0 for e in self.eng}
        self.waited = {e: {} for e in self.eng}
        self.lastw = {}
        self.readers = {}
        self.dsem = [nc.alloc_semaphore("dsem%d" % i) for i in range(self.NDMA)]
        self.dval = [0] * self.NDMA
        self.drr = 0
        self.cc_sem = nc.alloc_semaphore("cc_sem")
        self.cc_val = 0
        self._misc = MISC_OFF
        self._names = 0

    def sb(self, shape, dt, off):
        self._names += 1
        return self.nc.alloc_sbuf_tensor_at("t%d" % self._names, list(shape), dt, offset=off).ap()

    def misc(self, shape, dt):
        n = int(np.prod(shape[1:])) * (4 if dt == F32 else 2)
        n = (n + 31) // 32 * 32
        off = self._misc
        self._misc += n
        assert self._misc <= 229376, self._misc
        return self.sb(shape, dt, off)

    def _wait(self, e, tok):
        kind, ident, val = tok
        if kind == "eng":
            if ident == e and e == "pe":
                return
            sem = self.prog[ident]
        else:
            sem = self.dsem[ident]
        key = (kind, ident)
        if self.waited[e].get(key, 0) >= val:
            return
        self.eng[e].wait_ge(sem, val)
        self.waited[e][key] = val

    def _deps(self, e, r, w):
        for k in r:
            t = self.lastw.get(k)
            if t is not None:
                self._wait(e, t)
        for k in w:
            t = self.lastw.get(k)
            if t is not None:
                self._wait(e, t)
            for t in self.readers.get(k, {}).values():
                self._wait(e, t)

    def _record(self, tok, r, w):
        src = (tok[0], tok[1])
        for k in r:
            self.readers.setdefault(k, {})[src] = tok
        for k in w:
            self.lastw[k] = tok
            self.readers[k] = {}

    def op(self, e, fn, r=(), w=()):
        self._deps(e, r, w)
        ins = fn(self.eng[e])
        self.cnt[e] += 1
        ins.then_inc(self.prog[e], 1)
        self._record(("eng", e, self.cnt[e]), r, w)
        return ins

    def dma(self, q, out, in_, r=(), w=(), **kw):
        self._deps(q, r, w)
        i = self.drr
        self.drr = (self.drr + 1) % self.NDMA
        if self.dval[i] > 0:
            self._wait(q, ("dma", i, self.dval[i]))
        self.dval[i] += 16
        ins = self.eng[q].dma_start(out=out, in_=in_, **kw)
        ins.then_inc(self.dsem[i], 16)
        self._record(("dma", i, self.dval[i]), r, w)
        return ins

    def allgather_pairs(self, in_ap, out_ap, r=(), w=()):
        self._deps("pool", r, w)
        g = self.eng["pool"]
        g.collective_compute("AllGather", ALU.bypass, replica_groups=[[0, 1], [2, 3], [4, 5], [6, 7]],
                             ins=[in_ap], outs=[out_ap]).then_inc(self.cc_sem)
        self.cc_val += 1
        g.wait_ge(self.cc_sem, self.cc_val)
        self.op("pool", lambda e: e.memset(self.dummy, 0.0), r=r, w=list(w) + ["dummy"])

    def barrier(self):
        for e in self.eng:
            for o in self.eng:
                if o != e and self.cnt[o] > 0:
                    self._wait(e, ("eng", o, self.cnt[o]))
            for i in range(self.NDMA):
                if self.dval[i] > 0:
                    self._wait(e, ("dma", i, self.dval[i]))
        self.lastw = {}
        self.readers = {}


def bc(ap, shape):
    return ap.broadcast_to(list(shape))


class Prog(KB):
    def __init__(self, stages, dumps=()):
        super().__init__()
        nc = self.nc
        self.stages = stages
        self.dumps = dict()
        self.in_names = []
        need = set()
        for st in stages:
            if st == "memprep":
                continue
            need.add(st if isinstance(st, tuple) else (st,))

        def di(name, shape, dt=F32):
            self.in_names.append(name)
            return nc.dram_tensor(name, list(shape), dt, kind="ExternalInput").ap()

        def dil(kind, name, shape, layers):
            return {l: di("%s_%d" % (name, l), shape) for l in layers if (kind, l) in need}

        self.d_xT = di("xT", [D, NTOK])
        self.d_memT = di("memT", [D, NMEM])
        self.d_prm = di("prm", [128, NPRM])
        self.d_cst = di("cst", [128, 288])
        self.d_hmask = di("hmask", [1, 1152])
        self.d_maskT = di("maskT", [128, 640])
        self.d_selc = di("selc", [128, 1920 + 2160])
        self.d_s5lam = dil("mixa", "s5lam", [128, 3, 64], range(NA))
        self.d_s5bc = dil("mixa", "s5bc", [4, 128, 64, 16], range(NA))
        self.d_win = dil("mixa", "a_w_in", [16, 128, 16, 128], range(NA))
        self.d_wglu = dil("mixa", "a_w_glu", [16, 128, 16, 256], range(NA))
        if ("kv",) in need:
            self.d_wk = di("w_k", [16, 128, 16, 128])
            self.d_wv = di("w_v", [8, 128, 16, 256])
        self.d_bwq = dil("mixb", "b_w_q", [16, 128, 16, 128], range(NA, DEPTH))
        self.d_bwo = dil("mixb", "b_w_o", [16, 128, 16, 128], range(NA, DEPTH))
        self.d_bias = dil("mixb", "b_bias", [16, 128, 640], range(NA, DEPTH))
        self.d_mwq = dil("mem", "m_w_q", [4, 128, 16, 128], range(DEPTH))
        self.d_mwk = dil("mem", "m_w_k", [4, 128, 16, 128], range(DEPTH))
        self.d_mwv = dil("mem", "m_w_v", [2, 128, 16, 256], range(DEPTH))
        self.d_mwo = dil("mem", "m_w_o", [16, 128, 4, 128], range(DEPTH))
        self.d_wup = dil("ffn", "f_w_up", [NFT, 128, 16, 256], range(DEPTH))
        self.d_wdn = dil("ffn", "f_w_down", [16, 2, 128, 22, 128], range(DEPTH))
        self.d_out = nc.dram_tensor("outT", [D, NTOK], F32, kind="ExternalOutput").ap()
        for name, shape in dumps:
            self.dumps[name] = nc.dram_tensor("dbg_" + name, list(shape), F32, kind="ExternalOutput").ap()
        dt_ = lambda name, shape, dt: nc.dram_tensor(name, list(shape), dt).ap()
        self.kbuf = dt_("kbuf", [D, 1536], BF16)
        self.vbuf = dt_("vbuf", [1536, D], BF16)
        self.kx_in = dt_("kx_in", [D, 512], BF16)
        self.kx_out = dt_("kx_out", [2 * D, 512], BF16)
        self.vx_in = dt_("vx_in", [512, D], BF16)
        self.vx_out = dt_("vx_out", [1024, D], BF16)
        self.hh_in = dt_("hh_in", [128, 32], BF16)
        self.hh_out = dt_("hh_out", [256, 32], BF16)
        self.xs_in = dt_("xs_in", [128, 128], F32)
        self.xs_out = dt_("xs_out", [256, 128], F32)
        self.tb_wo = dt_("tb_wo", [8, 128, 2, 8, 128], BF16)
        self.tb_toep = dt_("tb_toep", [8, 128, 16, 128], BF16)

        self.xT = self.sb([128, KT, NTOK], F32, XT_OFF)
        self.P1 = self.sb([128, 16384], BF16, P1_OFF)
        self.P2 = self.sb([128, 22528], BF16, P2_OFF)
        self.P2f = self.sb([128, 11264], F32, P2_OFF)
        self.Tb = self.sb([128, 16384], BF16, T_OFF)
        self.Tf = self.sb([128, 8192], F32, T_OFF)
        self.prm = self.sb([128, NPRM], F32, PRM_OFF)
        self.memn = self.sb([128, KT, NMEM], BF16, MEMN_OFF)
        self.rstd = self.misc([128, 512], F32)
        self.sqt = [self.misc([128, 512], BF16) for _ in range(2)]
        self.ctmp = [self.misc([128, 512], F32) for _ in range(3)]
        self.ident_bf = self.misc([128, 128], BF16)
        self.ones_bf = self.misc([128, 128], BF16)
        self.cst = self.misc([128, 288], F32)
        self.ident_f = self.cst[:, 0:128]
        self.cmask = self.cst[:, 128:256]
        self.kvph = self.cst[:, 256:288]
        self.hh = self.misc([128, 32], BF16)
        self.hh0 = self.misc([128, 32], BF16)
        self.convhalo = self.misc([128, 88, 2], F32)
        self.small = self.misc([128, 64], F32)
        self.maskrow = self.misc([1, 1152], BF16)
        self.dummy = self.misc([128, 8], F32)
        self.xs5 = [self.misc([128, 64, 2], F32) for _ in range(4)]
        self.xs5_ar = self.misc([128, 64], F32)
        self.xs5_ai2 = self.misc([128, 64, 2], F32)
        self.negpi = self.misc([128, 1], F32)
        self.s5f = self.misc([128, 128], F32)
        self.s5i = self.misc([128, 128], mybir.dt.int32)
        self.psall = nc.alloc_psum_tensor("psall", [128, 4096], F32).ap()
        self.ps2 = [self.psall[:, 1024 * i:1024 * i + 1024] for i in range(4)]
        self.sq_i = 0
        self.wrr = 0
        self.build()

    def ps(self, i):
        return self.ps2[i // 2][:, (i % 2) * 512:(i % 2) * 512 + 512]

    def pskey(self, i):
        return ("ps", i)

    def wload(self, dram_ap):
        s = self.wrr
        self.wrr = (self.wrr + 1) % 4
        a, b = dram_ap.shape[1], dram_ap.shape[2]
        assert a * b <= 4096
        slot = self.Tb[:, s * 4096:s * 4096 + a * b].rearrange("p (a b) -> p a b", a=a)
        self.dma("pool", slot, dram_ap, w=[("w", s)])
        return slot, ("w", s)

    def dump(self, name, ap, keys):
        if name in self.dumps:
            self.barrier()
            self.dma("pool", self.dumps[name], ap, w=[("dump", name)])
            self.barrier()

    def gain(self, col, k):
        return self.prm[:, col + k:col + k + 1]

    def ss_acc(self, src, keys, first, last, n):
        i = self.sq_i
        self.sq_i ^= 1
        sq = self.sqt[i][:, :n]
        self.op("act", lambda e: e.activation(out=sq, in_=src, func=AF.Square), r=keys, w=[("sqt", i)] + [k_ for k_ in keys if k_[0] == "ps"])
        self.op("pe", lambda e: e.matmul(self.ps(7)[:, :n], lhsT=self.ones_bf, rhs=sq, start=first, stop=last),
                r=[("sqt", i), "ones"], w=[self.pskey(7)])

    def rstd_fin(self, n, dim=D):
        r = self.rstd[:, :n]
        self.op("act", lambda e: e.activation(out=r, in_=self.ps(7)[:, :n], func=AF.Sqrt, bias=self.epsc, scale=1.0 / dim),
                r=[self.pskey(7), "cst"], w=["rstd"])
        self.op("dve", lambda e: e.reciprocal(out=r, in_=r), r=["rstd"], w=["rstd"])

    def norm_pre(self, c0, n, gcol_, out_fn, out_keyfn, xkeys):
        for k in range(KT):
            self.ss_acc(self.xT[:, k, c0:c0 + n], [xkeys(k)], k == 0, k == KT - 1, n)
        self.rstd_fin(n)
        for k in range(KT):
            self.op("dve", lambda e: e.scalar_tensor_tensor(out=out_fn(k), in0=self.xT[:, k, c0:c0 + n], scalar=self.gain(gcol_, k),
                                                            in1=self.rstd[:, :n], op0=ALU.mult, op1=ALU.mult),
                    r=[xkeys(k), "rstd", "prm"], w=[out_keyfn(k)])

    def post_apply(self, c0, n, gcol_, m_fn, m_keyfn, xkeys):
        self.rstd_fin(n)
        for k in range(KT):
            t = self.ctmp[k % 3][:, :n]
            self.op("pool", lambda e: e.tensor_tensor(out=t, in0=m_fn(k), in1=self.rstd[:, :n], op=ALU.mult),
                    r=[m_keyfn(k), "rstd"], w=[("ctmp", k % 3)])
            xk = self.xT[:, k, c0:c0 + n]
            self.op("dve", lambda e: e.scalar_tensor_tensor(out=xk, in0=t, scalar=self.gain(gcol_, k), in1=xk,
                                                            op0=ALU.mult, op1=ALU.add),
                    r=[("ctmp", k % 3), "prm", xkeys(k)], w=[xkeys(k)])

    def attn_tile(self, slot, qT, qkeys, kT, kkeys, nk, v_fn, vkeys, out_ap, out_keys, bias=None, bkeys=(), mask=None):
        PA = self.psall
        if nk > 256:
            base = 1024 * slot
            S = PA[:, base:base + nk]
            TP = PA[:, base + 640:base + 640 + nk // 2].bitcast(BF16)
            O = PA[:, base + 512:base + 640]
        else:
            base = 512 * slot
            S = PA[:, base:base + nk]
            TP = PA[:, base + 256:base + 256 + nk // 2].bitcast(BF16)
            O = PA[:, base + 384:base + 512]
        sk = [("aslot", slot)]
        chunks = [(0, min(nk, 512))] + ([(512, nk)] if nk > 512 else [])
        i = slot
        mx = self.small[:, 4 * i:4 * i + 1]
        sm = self.small[:, 4 * i + 1:4 * i + 2]
        rs = self.small[:, 4 * i + 2:4 * i + 3]
        smk = ("small", i)
        Pf = self.Pf[i][:, :nk]
        PT = self.PT[i][:, :nk]
        nkt = nk // 128
        qkeys, kkeys, bkeys, vkeys, out_keys = list(qkeys), list(kkeys), list(bkeys), list(vkeys), list(out_keys)

        def st_scores():
            for (a, b) in chunks:
                last_plain = bias is None and mask is None
                self.op("pe", lambda e: e.matmul(S[:, a:b], lhsT=qT, rhs=kT[:, a:b], start=True, stop=last_plain),
                        r=qkeys + kkeys, w=sk)
                if bias is not None:
                    self.op("pe", lambda e: e.matmul(S[:, a:b], lhsT=self.ident_bf, rhs=bias[:, a:b], start=False, stop=(mask is None)),
                            r=bkeys + ["ident"], w=sk)
                if mask is not None:
                    self.op("pe", lambda e: e.matmul(S[:, a:b], lhsT=self.ones_bf[0:1, :], rhs=mask[:, a:b], start=False, stop=True),
                            r=["ones", "maskrow"], w=sk)

        def st_max():
            self.op("dve", lambda e: e.reduce_max(out=mx, in_=S, axis=AX.X), r=[], w=sk + [smk])
            self.op("dve", lambda e: e.tensor_scalar(out=mx, in0=mx, scalar1=-1.0, scalar2=None, op0=ALU.mult), r=[smk], w=[smk])

        def st_exp():
            self.op("act", lambda e: e.activation(out=Pf, in_=S, func=AF.Exp, bias=mx, scale=1.0), r=[smk], w=sk + [("Pf", i)])

        def st_norm():
            self.op("dve", lambda e: e.reduce_sum(out=sm, in_=Pf, axis=AX.X), r=[("Pf", i)], w=[smk])
            self.op("dve", lambda e: e.reciprocal(out=rs, in_=sm), r=[smk], w=[smk])
            self.op("dve", lambda e: e.tensor_scalar(out=Pf, in0=Pf, scalar1=rs, scalar2=None, op0=ALU.mult), r=[smk], w=[("Pf", i)])

        def st_tr():
            for kt in range(nkt):
                self.op("pe", lambda e: e.transpose(out=TP[:, kt * 128:(kt + 1) * 128], in_=Pf[:, kt * 128:(kt + 1) * 128], identity=self.ident_bf),
                        r=[("Pf", i), "ident"], w=sk)

        def st_ptcopy():
            self.op("act", lambda e: e.copy(out=PT, in_=TP), r=[], w=sk + [("PT", i)])

        def st_pv():
            for kt in range(nkt):
                self.op("pe", lambda e: e.matmul(O, lhsT=v_fn(kt), rhs=PT[:, kt * 128:(kt + 1) * 128], start=(kt == 0), stop=(kt == nkt - 1)),
                        r=[("PT", i)] + vkeys, w=sk)

        def st_out():
            self.op("act", lambda e: e.copy(out=out_ap, in_=O), r=[], w=sk + out_keys)

        return [st_scores, st_max, st_exp, st_norm, st_tr, st_ptcopy, st_pv, st_out]

    def attn_run(self, tiles, groups):
        nst = len(groups)
        live = {}
        for step in range(len(tiles) + nst - 1):
            for sidx in range(nst - 1, -1, -1):
                t = step - sidx
                if 0 <= t < len(tiles):
                    if sidx == 0:
                        live[t] = tiles[t]()
                    for prim in groups[sidx]:
                        live[t][prim]()
                    if sidx == nst - 1:
                        del live[t]

    def alloc_attn_bufs(self, off_bytes):
        o = off_bytes
        self.Pb = []
        self.PT = []
        for i in range(2):
            self.Pb.append(self.sb([128, 640], BF16, T_OFF + o)); o += 1280
            self.PT.append(self.sb([128, 640], BF16, T_OFF + o)); o += 1280
        self.Pf = [self.sb([128, 640], F32, T_OFF + o), self.sb([128, 640], F32, T_OFF + o + 2560)]
        o += 5120
        return o

    def setup(self):
        self.dma("sp", self.prm, self.d_prm, w=["prm"])
        self.dma("sp", self.cst, self.d_cst, w=["cst"])
        self.dma("pool", self.maskrow, self.d_hmask, w=["maskrow"])
        for k in range(KT):
            self.dma("sp", self.xT[:, k, :], self.d_xT[k * 128:(k + 1) * 128, :], w=[("x", k, 0), ("x", k, 1)])
        self.op("dve", lambda e: e.tensor_copy(out=self.ident_bf, in_=self.ident_f), r=["cst"], w=["ident"])
        self.op("dve", lambda e: e.memset(self.ones_bf, 1.0), w=["ones"])
        self.epsc = self.misc([128, 1], F32)
        self.op("dve", lambda e: e.memset(self.epsc, EPS), w=["cst2"])
        self.op("dve", lambda e: e.memset(self.negpi, -PI), w=["cst3"])
        self.barrier()

    def xk(self, k, b=None):
        return ("x", k, b)

    def mem_prepare(self):
        st = self.P2f[:, 0:KT * NMEM].rearrange("p (k t) -> p k t", k=KT)
        self.dma("sp", st, self.d_memT.rearrange("(k p) t -> p k t", p=128), w=["memst"])
        for k in range(KT):
            self.ss_acc(st[:, k, :], ["memst"], k == 0, k == KT - 1, NMEM)
        self.rstd_fin(NMEM)
        for k in range(KT):
            self.op("dve", lambda e: e.scalar_tensor_tensor(out=self.memn[:, k, :], in0=st[:, k, :], scalar=self.gain(G_MEMIN, k),
                                                            in1=self.rstd[:, :NMEM], op0=ALU.mult, op1=ALU.mult),
                    r=["memst", "rstd", "prm"], w=["memn"])
        self.barrier()

    def mem_attn(self, l):
        P1, P2 = self.P1, self.P2
        H = P2[:, 0:16384].rearrange("p (k t) -> p k t", k=KT)
        Q = P1[:, 0:4096].rearrange("p (h t) -> p h t", h=4)
        O = P1[:, 4096:8192].rearrange("p (h t) -> p h t", h=4)
        M = H
        KM = P1[:, 8192:9216].rearrange("p (h t) -> p h t", h=4)
        VM = P1[:, 9216:10240].rearrange("p (m f) -> p m f", m=2)
        base = P1_OFF + 10240 * 2
        self.Pf = [self.sb([128, 256], BF16, base + i * 1024) for i in range(8)]
        self.PT = [self.sb([128, 256], BF16, base + i * 1024 + 512) for i in range(8)]
        for b in range(2):
            self.norm_pre(b * 512, 512, gcol(l, G_MEM0), lambda k: H[:, k, b * 512:(b + 1) * 512], lambda k: ("H", k, b), lambda k: ("x", k, b))
        for h in range(4):
            slot, wk = self.wload(self.d_mwk[l][h])
            b = h % 2
            for k in range(KT):
                self.op("pe", lambda e: e.matmul(self.ps(b)[:, :NMEM], lhsT=slot[:, k, :], rhs=self.memn[:, k, :], start=(k == 0), stop=(k == KT - 1)),
                        r=[wk, "memn"], w=[self.pskey(b)])
            self.op("act", lambda e: e.copy(out=KM[:, h, :], in_=self.ps(b)[:, :NMEM]), r=[self.pskey(b)], w=["KM"])
        for vt in range(2):
            slot, wk = self.wload(self.d_mwv[l][vt])
            for mt in range(2):
                b = 2 + mt
                for k in range(KT):
                    self.op("pe", lambda e: e.matmul(self.ps(b)[:, :256], lhsT=self.memn[:, k, mt * 128:(mt + 1) * 128], rhs=slot[:, k, :],
                                                     start=(k == 0), stop=(k == KT - 1)),
                            r=[wk, "memn"], w=[self.pskey(b)])
                self.op("act", lambda e: e.copy(out=VM[:, mt, vt * 256:(vt + 1) * 256], in_=self.ps(b)[:, :256]), r=[self.pskey(b)], w=["VM"])
        sc = 128.0 ** -0.5
        for h in range(4):
            slot, wk = self.wload(self.d_mwq[l][h])
            for b in range(2):
                pb = 4 + b
                for k in range(KT):
                    self.op("pe", lambda e: e.matmul(self.ps(pb), lhsT=slot[:, k, :], rhs=H[:, k, b * 512:(b + 1) * 512], start=(k == 0), stop=(k == KT - 1)),
                            r=[wk, ("H", k, b)], w=[self.pskey(pb)])
                self.op("act", lambda e: e.mul(out=Q[:, h, b * 512:(b + 1) * 512], in_=self.ps(pb), mul=sc), r=[self.pskey(pb)], w=[("Q", h, b)])
        self.barrier()
        self.dump('H', P2[:, 0:16384], [])
        self.dump('Q', P1[:, 0:4096], [])
        self.dump('KM', P1[:, 8192:9216], [])
        self.dump('VM', P1[:, 9216:10240], [])
        if CUT <= 2:
            return
        tiles = []
        for h in range(4):
            for i in range(8):
                n = len(tiles)
                tiles.append(lambda h=h, i=i, n=n: self.attn_tile(
                    n % 8, Q[:, h, i * 128:(i + 1) * 128], [("Q", h, i // 4)], KM[:, h, :], ["KM"], NMEM,
                    lambda kt: VM[:, kt, h * 128:(h + 1) * 128], ["VM"], O[:, h, i * 128:(i + 1) * 128], [("O", h, i // 4)]))
        self.attn_run(tiles, [[0], [1], [2], [3], [4], [5], [6], [7]])
        self.barrier()
        self.dump('O', P1[:, 4096:8192], [])
        if CUT <= 4:
            return
        for t in range(16):
            slot, wk = self.wload(self.d_mwo[l][t])
            for b in range(2):
                pb = (2 * t + b) % 6
                for k in range(4):
                    self.op("pe", lambda e: e.matmul(self.ps(pb), lhsT=slot[:, k, :], rhs=O[:, k, b * 512:(b + 1) * 512], start=(k == 0), stop=(k == 3)),
                            r=[wk, ("O", k, b)], w=[self.pskey(pb)])
                self.ss_acc_b(b, self.ps(pb), [self.pskey(pb)], t == 0, t == 15)
                self.op("dve", lambda e: e.tensor_copy(out=M[:, t, b * 512:(b + 1) * 512], in_=self.ps(pb)), r=[self.pskey(pb)], w=[("M", t, b), self.pskey(pb)])
        self.dump('M', P2[:, 0:16384], [])
        for b in range(2):
            self.post_apply_b(b, gcol(l, G_MEM1), lambda k: M[:, k, b * 512:(b + 1) * 512], lambda k: ("M", k, b))
        self.barrier()

    def ss_acc_b(self, b, src, keys, first, last, n=512):
        i = self.sq_i
        self.sq_i ^= 1
        sq = self.sqt[i][:, :n]
        bank = 7 - b
        self.op("act", lambda e: e.activation(out=sq, in_=src, func=AF.Square), r=keys, w=[("sqt", i)] + [k_ for k_ in keys if k_[0] == "ps"])
        self.op("pe", lambda e: e.matmul(self.ps(bank)[:, :n], lhsT=self.ones_bf, rhs=sq, start=first, stop=last),
                r=[("sqt", i), "ones"], w=[self.pskey(bank)])

    def post_apply_b(self, b, gcol_, m_fn, m_keyfn, n=512):
        bank = 7 - b
        r_ = self.rstd[:, :n]
        self.op("act", lambda e: e.activation(out=r_, in_=self.ps(bank)[:, :n], func=AF.Sqrt, bias=self.epsc, scale=1.0 / D),
                r=[self.pskey(bank), "cst2"], w=["rstd"])
        self.op("dve", lambda e: e.reciprocal(out=r_, in_=r_), r=["rstd"], w=["rstd"])
        c0 = b * 512
        for k in range(KT):
            t = self.ctmp[k % 3][:, :n]
            self.op("pool", lambda e: e.tensor_tensor(out=t, in0=m_fn(k), in1=r_, op=ALU.mult),
                    r=[m_keyfn(k), "rstd"], w=[("ctmp", k % 3)])
            xk = self.xT[:, k, c0:c0 + n]
            self.op("dve", lambda e: e.scalar_tensor_tensor(out=xk, in0=t, scalar=self.gain(gcol_, k), in1=xk,
                                                            op0=ALU.mult, op1=ALU.add),
                    r=[("ctmp", k % 3), "prm", ("x", k, b)], w=[("x", k, b)])

    def ffn(self, l):
        P1, P2 = self.P1, self.P2
        HB = P1[:, 0:8192].rearrange("p (k t) -> p k t", k=KT)
        M = P1[:, 8192:16384].rearrange("p (k t) -> p k t", k=KT)
        ACT_ = P2[:, 0:NFT * 512].rearrange("p (j t) -> p j t", j=NFT)
        hh3 = self.hh.rearrange("p (k t) -> p k t", k=KT)
        hl = self.hh0.rearrange("p (k t) -> p k t", k=KT)
        self.norm_pre(1022, 2, gcol(l, G_FFN0), lambda k: hl[:, k, :], lambda k: "hh0", lambda k: ("x", k, 1))
        self.dma("sp", self.hh_in, self.hh0, r=["hh0"], w=["hh_in"])
        self.allgather_pairs(self.hh_in, self.hh_out, r=["hh_in"], w=["hh_out"])
        self.dma("sp", self.hh0, self.hh_out[0:128, :], r=["hh_out"], w=["hh0"])
        self.op("dve", lambda e: e.tensor_scalar(out=self.hh, in0=self.hh0, scalar1=self.prm[:, ISSEC:ISSEC + 1], scalar2=None, op0=ALU.mult),
                r=["hh0", "prm"], w=["hh"])
        cw = lambda tap, tile: self.prm[:, CONV0 + l * 352 + tap * 88 + tile:CONV0 + l * 352 + tap * 88 + tile + 1]
        def up_proj(b):
                for j in range(NFT):
                    slot, wk = self.wload(self.d_wup[l][j])
                    s3 = (j % 2) * 3
                    pv, pg, ph = self.ps(s3), self.ps(s3 + 1), self.ps(s3 + 2)
                    kv_, kg_, kh_ = self.pskey(s3), self.pskey(s3 + 1), self.pskey(s3 + 2)
                    for k in range(KT):
                        self.op("pe", lambda e: e.matmul(pv, lhsT=slot[:, k, 0:128], rhs=HB[:, k, :], start=(k == 0), stop=(k == KT - 1)),
                                r=[wk, ("HB", k)], w=[kv_])
                    for k in range(KT):
                        self.op("pe", lambda e: e.matmul(pg, lhsT=slot[:, k, 128:256], rhs=HB[:, k, :], start=(k == 0), stop=(k == KT - 1)),
                                r=[wk, ("HB", k)], w=[kg_])
                    if b == 0:
                        for k in range(KT):
                            self.op("pe", lambda e: e.matmul(ph[:, 0:2], lhsT=slot[:, k, 0:128], rhs=hh3[:, k, :], start=(k == 0), stop=(k == KT - 1)),
                                    r=[wk, "hh"], w=[kh_])
                        for k in range(KT):
                            self.op("pe", lambda e: e.matmul(ph[:, 2:4], lhsT=slot[:, k, 128:256], rhs=hh3[:, k, :], start=(k == 0), stop=(k == KT - 1)),
                                    r=[wk, "hh"], w=[kh_])
                    accs = []
                    for vi, (pp, pk, tile) in enumerate(((pv, kv_, j), (pg, kg_, NFT + j))):
                        t = self.ctmp[vi]
                        tk = ("ctmp", vi)
                        self.op("act", lambda e: e.activation(out=t, in_=pp, func=AF.Identity, bias=cw(3, tile), scale=cw(2, tile)),
                                r=[pk, "prm"], w=[tk, pk])
                        self.op("dve", lambda e: e.scalar_tensor_tensor(out=t[:, 1:512], in0=pp[:, 0:511], scalar=cw(1, tile), in1=t[:, 1:512],
                                                                        op0=ALU.mult, op1=ALU.add), r=[pk, "prm", tk], w=[tk, pk])
                        self.op("dve", lambda e: e.scalar_tensor_tensor(out=t[:, 2:512], in0=pp[:, 0:510], scalar=cw(0, tile), in1=t[:, 2:512],
                                                                        op0=ALU.mult, op1=ALU.add), r=[pk, "prm", tk], w=[tk, pk])
                        if b == 0:
                            hsrc = ph[:, 2 * vi:2 * vi + 2]
                            hkeys = [kh_]
                        else:
                            hsrc = self.convhalo[:, tile, :]
                            hkeys = [("chalo", tile)]
                        self.op("dve", lambda e: e.scalar_tensor_tensor(out=t[:, 0:2], in0=hsrc, scalar=cw(0, tile), in1=t[:, 0:2],
                                                                        op0=ALU.mult, op1=ALU.add), r=hkeys + ["prm", tk], w=[tk] + [k_ for k_ in hkeys if k_[0] == "ps"])
                        self.op("dve", lambda e: e.scalar_tensor_tensor(out=t[:, 0:1], in0=hsrc[:, 1:2], scalar=cw(1, tile), in1=t[:, 0:1],
                                                                        op0=ALU.mult, op1=ALU.add), r=hkeys + ["prm", tk], w=[tk] + [k_ for k_ in hkeys if k_[0] == "ps"])
                        if b == 0:
                            self.op("act", lambda e: e.copy(out=self.convhalo[:, tile, :], in_=pp[:, 510:512]), r=[pk], w=[("chalo", tile), pk])
                        accs.append((t, tk))
                    (tv, tvk), (tg, tgk) = accs
                    sg = self.ctmp[2]
                    self.op("act", lambda e: e.activation(out=sg, in_=tg, func=AF.Silu), r=[tgk], w=[("ctmp", 2)])
                    self.op("dve", lambda e: e.tensor_tensor(out=ACT_[:, j, :], in0=tv, in1=sg, op=ALU.mult), r=[tvk, ("ctmp", 2)], w=[("ACT", j)])

        def down_proj(b):
                for i in range(16):
                    pb = i % 6
                    for half in range(2):
                        slot, wk = self.wload(self.d_wdn[l][i, half])
                        for jj in range(22):
                            j = half * 22 + jj
                            self.op("pe", lambda e: e.matmul(self.ps(pb), lhsT=slot[:, jj, :], rhs=ACT_[:, j, :], start=(j == 0), stop=(j == NFT - 1)),
                                    r=[wk, ("ACT", j)], w=[self.pskey(pb)])
                    self.ss_acc(self.ps(pb), [self.pskey(pb)], i == 0, i == 15, 512)
                    self.op("dve", lambda e: e.tensor_copy(out=M[:, i, :], in_=self.ps(pb)), r=[self.pskey(pb)], w=[("M", i), self.pskey(pb)])

        def norm_blk(b):
            self.norm_pre(b * 512, 512, gcol(l, G_FFN0), lambda k: HB[:, k, :], lambda k: ("HB", k), lambda k: ("x", k, b))

        def post_blk(b):
            self.post_apply(b * 512, 512, gcol(l, G_FFN1), lambda k: M[:, k, :], lambda k: ("M", k), lambda k: ("x", k, b))

        norm_blk(0)
        up_proj(0)
        norm_blk(1)
        down_proj(0)
        post_blk(0)
        up_proj(1)
        down_proj(1)
        post_blk(1)
        self.barrier()

    def kv_phase(self):
        P1, P2 = self.P1, self.P2
        H = P2[:, 0:16384].rearrange("p (k t) -> p k t", k=KT)
        for b in range(2):
            self.norm_pre(b * 512, 512, G_KV, lambda k: H[:, k, b * 512:(b + 1) * 512], lambda k: ("H", k, b), lambda k: ("x", k, b))
        KS = [P1[:, i * 1024:(i + 1) * 1024] for i in range(2)]
        for h in range(16):
            slot, wk = self.wload(self.d_wk[h])
            st, sk = KS[h % 2], ("KS", h % 2)
            for b in range(2):
                pb = (2 * h + b) % 6
                for k in range(KT):
                    self.op("pe", lambda e: e.matmul(self.ps(pb), lhsT=slot[:, k, :], rhs=H[:, k, b * 512:(b + 1) * 512], start=(k == 0), stop=(k == KT - 1)),
                            r=[wk, ("H", k, b)], w=[self.pskey(pb)])
                self.op("act", lambda e: e.copy(out=st[:, b * 512:(b + 1) * 512], in_=self.ps(pb)), r=[self.pskey(pb)], w=[sk])
            self.dma("sp", self.kbuf[h * 128:(h + 1) * 128, 512:1536], st, r=[sk], w=[("kbuf", h)])
            self.dma("sp", self.kx_in[h * 128:(h + 1) * 128, :], st[:, 512:1024], r=[sk], w=[("kx_in", h)])
        VS = [P1[:, 2048 + i * 256:2048 + (i + 1) * 256] for i in range(2)]
        n = 0
        for ct in range(8):
            slot, wk = self.wload(self.d_wv[ct])
            for tt in range(8):
                pb = n % 6
                st, sk = VS[n % 2], ("VS", n % 2)
                n += 1
                for k in range(KT):
                    self.op("pe", lambda e: e.matmul(self.ps(pb)[:, :256], lhsT=H[:, k, tt * 128:(tt + 1) * 128], rhs=slot[:, k, :], start=(k == 0), stop=(k == KT - 1)),
                            r=[wk, ("H", k, tt // 4)], w=[self.pskey(pb)])
                self.op("act", lambda e: e.copy(out=st, in_=self.ps(pb)[:, :256]), r=[self.pskey(pb)], w=[sk])
                self.dma("sp", self.vbuf[512 + tt * 128:512 + (tt + 1) * 128, ct * 256:(ct + 1) * 256], st, r=[sk], w=[("vbuf", n)])
                if tt >= 4:
                    self.dma("sp", self.vx_in[(tt - 4) * 128:(tt - 3) * 128, ct * 256:(ct + 1) * 256], st, r=[sk], w=[("vx_in", n)])
        self.barrier()
        self.allgather_pairs(self.kx_in, self.kx_out, w=["kx_out"])
        self.allgather_pairs(self.vx_in, self.vx_out, w=["vx_out"])
        self.dma("sp", self.kbuf[:, 0:512], self.kx_out[0:D, :], r=["kx_out"], w=["kbufh"])
        self.dma("sp", self.vbuf[0:512, :], self.vx_out[0:512, :], r=["vx_out"], w=["vbufh"])
        self.barrier()

    def mixer_b(self, l):
        P1, P2 = self.P1, self.P2
        H = P2[:, 0:16384].rearrange("p (k t) -> p k t", k=KT)
        Q = P1[:, 0:16384].rearrange("p (h t) -> p h t", h=16)
        O = H
        M = Q
        for b in range(2):
            self.norm_pre(b * 512, 512, gcol(l, G_MIX0), lambda k: H[:, k, b * 512:(b + 1) * 512], lambda k: ("H", k, b), lambda k: ("x", k, b))
        sc = 128.0 ** -0.5
        for h in range(16):
            slot, wk = self.wload(self.d_bwq[l][h])
            for b in range(2):
                pb = (2 * h + b) % 6
                for k in range(KT):
                    self.op("pe", lambda e: e.matmul(self.ps(pb), lhsT=slot[:, k, :], rhs=H[:, k, b * 512:(b + 1) * 512], start=(k == 0), stop=(k == KT - 1)),
                            r=[wk, ("H", k, b)], w=[self.pskey(pb)])
                self.op("act", lambda e: e.mul(out=Q[:, h, b * 512:(b + 1) * 512], in_=self.ps(pb), mul=sc), r=[self.pskey(pb)], w=[("Q", h, b)])
        self.barrier()
        o = T_OFF
        Kh = [self.sb([128, 1536], BF16, P2_OFF + 32768 + i * 3072) for i in range(3)]
        Vh = [self.sb([128, 12, 128], BF16, o + i * 3072) for i in range(3)]; o += 9216
        Bst = [self.sb([128, 640], F32, o)] * 3; o += 2560
        BT = [self.sb([128, 640], BF16, o + i * 1280) for i in range(3)]; o += 3840
        mT = self.sb([128, 640], F32, o); o += 2560
        self.Pf = [self.sb([128, 640], BF16, o + i * 1280) for i in range(4)]; o += 5120
        self.PT = [self.sb([128, 640], BF16, o + i * 1280) for i in range(4)]; o += 5120
        assert o <= T_OFF + 32768
        self.dma("sp", mT, self.d_maskT, w=["mT"])
        vsrc = self.vbuf.rearrange("(t p) f -> p t f", p=128)
        def load_head(h):
            s = h % 3
            self.dma("sp", Kh[s], self.kbuf[h * 128:(h + 1) * 128, :], w=[("Kh", s)])
            self.dma("sp", Vh[s], vsrc[:, :, h * 128:(h + 1) * 128], w=[("Vh", s)])
            self.dma("sp", Bst[s], self.d_bias[l][h], w=[("Bst", 0)])
            self.op("dve", lambda e: e.tensor_tensor(out=BT[s], in0=Bst[s], in1=mT, op=ALU.add), r=[("Bst", 0), "mT"], w=[("BT", s)])

        def mk(h, i, n):
            if h == 0 and i == 0:
                load_head(0)
            if i == 0 and h + 1 < 16:
                load_head(h + 1)
            s = h % 3
            return self.attn_tile(n % 4, Q[:, h, i * 128:(i + 1) * 128], [("Q", h, i // 4)], Kh[s][:, i * 128:i * 128 + 640], [("Kh", s)], 640,
                                  lambda kt: Vh[s][:, i + kt, :], [("Vh", s)], O[:, h, i * 128:(i + 1) * 128], [("O", h, i // 4)],
                                  bias=BT[s], bkeys=[("BT", s)], mask=(self.maskrow[0:1, i * 128:i * 128 + 640] if i < 4 else None))

        tiles = []
        for h in range(16):
            for i in range(8):
                n = len(tiles)
                tiles.append(lambda h=h, i=i, n=n: mk(h, i, n))
        self.attn_run(tiles, [[0], [1, 2], [3, 4], [5, 6], [7]])
        self.barrier()
        for t in range(16):
            slot, wk = self.wload(self.d_bwo[l][t])
            for b in range(2):
                pb = (2 * t + b) % 6
                for k in range(KT):
                    self.op("pe", lambda e: e.matmul(self.ps(pb), lhsT=slot[:, k, :], rhs=O[:, k, b * 512:(b + 1) * 512], start=(k == 0), stop=(k == KT - 1)),
                            r=[wk, ("O", k, b)], w=[self.pskey(pb)])
                self.ss_acc_b(b, self.ps(pb), [self.pskey(pb)], t == 0, t == 15)
                self.op("dve", lambda e: e.tensor_copy(out=M[:, t, b * 512:(b + 1) * 512], in_=self.ps(pb)), r=[self.pskey(pb)], w=[("M", t, b), self.pskey(pb)])
        for b in range(2):
            self.post_apply_b(b, gcol(l, G_MIX1), lambda k: M[:, k, b * 512:(b + 1) * 512], lambda k: ("M", k, b))
        self.barrier()

    def mixer_a(self, l):
        P1, P2 = self.P1, self.P2
        H = P2[:, 0:16384].rearrange("p (k t) -> p k t", k=KT)
        Uf = P1[:, 0:16384].rearrange("p (k t) -> p k t", k=KT)
        UG = P2[:, 0:16384].rearrange("p (g c) -> p g c", g=128)
        SX = P1[:, 0:16384].rearrange("p (m r c) -> p m r c", m=64, r=2)
        G_ = P2[:, 0:16384].rearrange("p (k t) -> p k t", k=KT)
        M = P1[:, 0:16384].rearrange("p (k t) -> p k t", k=KT)
        for b in range(2):
            self.norm_pre(b * 512, 512, gcol(l, G_MIX0), lambda k: H[:, k, b * 512:(b + 1) * 512], lambda k: ("H", k, b), lambda k: ("x", k, b))
        for t in range(16):
            slot, wk = self.wload(self.d_win[l][t])
            for b in range(2):
                pb = (2 * t + b) % 6
                for k in range(KT):
                    self.op("pe", lambda e: e.matmul(self.ps(pb), lhsT=slot[:, k, :], rhs=H[:, k, b * 512:(b + 1) * 512], start=(k == 0), stop=(k == KT - 1)),
                            r=[wk, ("H", k, b)], w=[self.pskey(pb)])
                self.op("act", lambda e: e.copy(out=Uf[:, t, b * 512:(b + 1) * 512], in_=self.ps(pb)), r=[self.pskey(pb)], w=[("P1", t)])
        self.barrier()
        SEL = self.sb([128, 4080], BF16, MISC_OFF)
        ZP = SEL[:, 0:1920].rearrange("p (g c) -> p g c", g=8)
        ZZ = SEL[:, 1920:4080]
        self.dma("pool", SEL.rearrange("p (a b) -> p a b", a=4), self.d_selc.rearrange("p (a b) -> p a b", a=4), w=["sel"])
        tabsets = []
        for j in range(2):
            d_ = {}
            for idx, nm in enumerate(("WSTre", "WSTim", "WOnre", "WOnim", "WS2re", "WS2im")):
                d_[nm] = self.sb([128, 8, 128], BF16, T_OFF + j * 12288 + idx * 2048)
            tabsets.append(d_)
        WOre = self.sb([128, 8, 128], BF16, T_OFF + 24576)
        WOim = self.sb([128, 8, 128], BF16, T_OFF + 24576 + 2048)
        WOpair = self.sb([128, 2, 8, 128], BF16, T_OFF + 24576)
        TOEP = self.sb([128, 16, 128], BF16, T_OFF + 28672)
        o = P2_OFF + 32768
        SCR0 = o
        EC = self.sb([128, 16, 8], F32, o); o += 512
        ES = self.sb([128, 16, 8], F32, o); o += 512
        BCs = [self.sb([128, 8, 16], F32, o + i * 512) for i in range(4)]; o += 2048
        BbR = self.sb([128, 8, 16], F32, o); o += 512
        BbI = self.sb([128, 8, 16], F32, o); o += 512
        tA = self.sb([128, 4, 8, 16], F32, o); o += 2048
        tB = self.sb([128, 4, 8, 16], F32, o); o += 2048
        LAM = self.sb([128, 3, 64], F32, o); o += 768
        DLT = self.sb([128, 3, 64], F32, o); o += 768
        FF = self.sb([128, 2, 64], F32, o); o += 512
        TS = [self.sb([128, 16, 8], F32, o + i * 512) for i in range(3)]; o += 1536
        ANG = self.sb([128, 16, 8], F32, o); o += 512
        assert o <= P2_OFF + 45056, o
        WSCR = o - 2048
        X = [self.xs5[0], self.xs5[1]]
        T1, T2, AR, AI2 = self.xs5[2], self.xs5[3], self.xs5_ar, self.xs5_ai2
        kv = self.kvph[:, 0:16]
        self.dma("sp", LAM, self.d_s5lam[l], w=["lam"])
        lr, li, ld = LAM[:, 0, :], LAM[:, 1, :], LAM[:, 2, :]
        dt, lrdt, th = DLT[:, 0, :], DLT[:, 1, :], DLT[:, 2, :]
        tt_ = lambda out, a, b, op, r=(), w=(): self.op("dve", lambda e: e.tensor_tensor(out=out, in0=a, in1=b, op=op), r=list(r), w=list(w))
        I32 = mybir.dt.int32

        def sin_(out, arg, phase, r, w):
            shp = list(arg.shape)
            if len(shp) == 2:
                fs, is_ = self.s5f[:, 0:shp[1]], self.s5i[:, 0:shp[1]]
            else:
                fs = self.s5f.rearrange("p (a b) -> p a b", a=shp[1])
                is_ = self.s5i.rearrange("p (a b) -> p a b", a=shp[1])
            k_ = ["sarg"]
            self.op("dve", lambda e: e.tensor_scalar(out=arg, in0=arg, scalar1=float(phase), scalar2=None, op0=ALU.add), r=list(r), w=k_)
            self.op("dve", lambda e: e.tensor_scalar(out=fs, in0=arg, scalar1=1.0 / (2 * PI), scalar2=None, op0=ALU.mult), r=k_, w=["sfs"])
            self.op("dve", lambda e: e.tensor_copy(out=is_, in_=fs), r=["sfs"], w=["sis"])
            self.op("dve", lambda e: e.tensor_copy(out=fs, in_=is_), r=["sis"], w=["sfs"])
            self.op("dve", lambda e: e.scalar_tensor_tensor(out=arg, in0=fs, scalar=-2 * PI, in1=arg, op0=ALU.mult, op1=ALU.add), r=["sfs"] + k_, w=k_)
            self.op("dve", lambda e: e.tensor_scalar(out=fs, in0=arg, scalar1=PI, scalar2=-2 * PI, op0=ALU.is_gt, op1=ALU.mult), r=k_, w=["sfs"])
            self.op("dve", lambda e: e.tensor_tensor(out=arg, in0=arg, in1=fs, op=ALU.add), r=["sfs"] + k_, w=k_)
            self.op("dve", lambda e: e.tensor_scalar(out=fs, in0=arg, scalar1=-PI, scalar2=2 * PI, op0=ALU.is_lt, op1=ALU.mult), r=k_, w=["sfs"])
            self.op("dve", lambda e: e.tensor_tensor(out=arg, in0=arg, in1=fs, op=ALU.add), r=["sfs"] + k_, w=k_)
            self.op("act", lambda e: e.activation(out=out, in_=arg, func=AF.Sin), r=k_, w=list(w))

        self.op("act", lambda e: e.activation(out=dt, in_=ld, func=AF.Exp), r=["lam"], w=["dlt"])
        tt_(lrdt, lr, dt, ALU.mult, ["lam", "dlt"], ["dlt"])
        tt_(th, li, dt, ALU.mult, ["lam", "dlt"], ["dlt"])
        W = [self.sb([128, 64], F32, WSCR + i * 256) for i in range(8)]
        c1, s1, mag, abr, abi, nr, den, w7 = W
        self.op("dve", lambda e: e.tensor_copy(out=c1, in_=th), r=["dlt"], w=["c1"])
        sin_(c1, c1, PI / 2, ["c1"], ["c1"])
        self.op("dve", lambda e: e.tensor_copy(out=s1, in_=th), r=["dlt"], w=["s1"])
        sin_(s1, s1, 0.0, ["s1"], ["s1"])
        self.op("act", lambda e: e.activation(out=mag, in_=lrdt, func=AF.Exp), r=["dlt"], w=["mag"])
        tt_(abr, mag, c1, ALU.mult, ["mag", "c1"], ["abr"])
        tt_(abi, mag, s1, ALU.mult, ["mag", "s1"], ["abi"])
        self.op("dve", lambda e: e.tensor_scalar(out=nr, in0=abr, scalar1=-1.0, scalar2=None, op0=ALU.add), r=["abr"], w=["nr"])
        tt_(den, lr, lr, ALU.mult, ["lam"], ["den"])
        tt_(w7, li, li, ALU.mult, ["lam"], ["w7"])
        tt_(den, den, w7, ALU.add, ["den", "w7"], ["den"])
        self.op("dve", lambda e: e.reciprocal(out=den, in_=den), r=["den"], w=["den"])
        fre, fim = FF[:, 0, :], FF[:, 1, :]
        tt_(fre, nr, lr, ALU.mult, ["nr", "lam"], ["fre"])
        tt_(w7, abi, li, ALU.mult, ["abi", "lam"], ["w7"])
        tt_(fre, fre, w7, ALU.add, ["fre", "w7"], ["fre"])
        tt_(fre, fre, den, ALU.mult, ["fre", "den"], ["fre"])
        tt_(fim, abi, lr, ALU.mult, ["abi", "lam"], ["fim"])
        tt_(w7, nr, li, ALU.mult, ["nr", "lam"], ["w7"])
        tt_(fim, fim, w7, ALU.subtract, ["fim", "w7"], ["fim"])
        tt_(fim, fim, den, ALU.mult, ["fim", "den"], ["fim"])
        self.barrier()
        if CUT == 51:
            return
        import functools

        def gen(gb):
            E = []
            add = lambda fn, *a: E.append(functools.partial(fn, *a))
            tabs = tabsets[gb % 2]
            st = gb % 2
            m0 = 8 * gb
            for i in range(4):
                add(lambda i=i: self.dma("sp", BCs[i], self.d_s5bc[l][i][:, m0:m0 + 8, :], w=[("bc", i)]))
            BR, BI, CR, CI = BCs
            kvb = bc(kv.unsqueeze(2), [128, 16, 8])
            thb = bc(th[:, m0:m0 + 8].unsqueeze(1), [128, 16, 8])
            lrb = bc(lrdt[:, m0:m0 + 8].unsqueeze(1), [128, 16, 8])
            Ct, St, Et = TS
            add(tt_, ANG, kvb, thb, ALU.mult, ["dlt"], ["ang"])
            add(lambda: self.op("dve", lambda e: e.tensor_copy(out=Ct, in_=ANG), r=["ang"], w=["Ct"]))
            add(sin_, Ct, Ct, PI / 2, ["Ct"], ["Ct"])
            add(sin_, ANG, ANG, 0.0, ["ang"], ["ang"])
            add(tt_, Et, kvb, lrb, ALU.mult, ["dlt"], ["Et"])
            add(lambda: self.op("act", lambda e: e.activation(out=Et, in_=Et, func=AF.Exp), r=["Et"], w=["Et"]))
            add(tt_, EC, Et, Ct, ALU.mult, ["Et", "Ct"], ["EC"])
            add(tt_, ES, Et, ANG, ALU.mult, ["Et", "ang"], ["ES"])
            add(lambda: self.op("act", lambda e: e.copy(out=AR[:, m0:m0 + 8], in_=EC[:, 15, :]), r=["EC"], w=["AR"]))
            add(lambda: self.op("act", lambda e: e.copy(out=AI2[:, m0:m0 + 8, 1], in_=ES[:, 15, :]), r=["ES"], w=["AI2"]))
            add(lambda: self.op("act", lambda e: e.mul(out=AI2[:, m0:m0 + 8, 0], in_=ES[:, 15, :], mul=-1.0), r=["ES"], w=["AI2"]))
            freb = bc(fre[:, m0:m0 + 8].unsqueeze(2), [128, 8, 16])
            fimb = bc(fim[:, m0:m0 + 8].unsqueeze(2), [128, 8, 16])
            a3 = tA[:, 0:1, :, :].rearrange("p a s h -> p (a s) h")
            b3 = tB[:, 0:1, :, :].rearrange("p a s h -> p (a s) h")
            add(tt_, a3, freb, BR, ALU.mult, [("bc", 0)], ["tA"])
            add(tt_, b3, fimb, BI, ALU.mult, [("bc", 1)], ["tB"])
            add(tt_, BbR, a3, b3, ALU.subtract, ["tA", "tB"], ["BbR"])
            add(tt_, a3, freb, BI, ALU.mult, [("bc", 1)], ["tA"])
            add(tt_, b3, fimb, BR, ALU.mult, [("bc", 0)], ["tB"])
            add(tt_, BbI, a3, b3, ALU.add, ["tA", "tB"], ["BbI"])
            rk = ["EC", "ES", "BbR", "BbI", ("bc", 2), ("bc", 3)]

            def table(dst, key, e1, x1, e2, x2, comb):
                add(tt_, tA, e1, x1, ALU.mult, rk, ["tA"])
                add(lambda: self.op("pool", lambda e: e.tensor_tensor(out=tB, in0=e2, in1=x2, op=ALU.mult), r=list(rk), w=["tB"]))
                if comb == "sub":
                    add(tt_, dst, tA, tB, ALU.subtract, ["tA", "tB"], [key])
                elif comb == "add":
                    add(tt_, dst, tA, tB, ALU.add, ["tA", "tB"], [key])
                else:
                    add(lambda: self.op("dve", lambda e: e.scalar_tensor_tensor(out=dst, in0=tA, scalar=-1.0, in1=tB, op0=ALU.mult, op1=ALU.subtract),
                                        r=["tA", "tB"], w=[key]))

            for hb in range(2):
                mh = 4 * hb

                def kview(tab, a_, b_, step, mh=mh):
                    sl = tab[:, a_:b_:step, mh:mh + 4]
                    return bc(sl.rearrange("p s m -> p m s").unsqueeze(3), [128, 4, 8, 16])

                def bview(t3, mh=mh):
                    return bc(t3[:, mh:mh + 4, :].unsqueeze(2), [128, 4, 8, 16])

                def ov(t, mh=mh):
                    return t[:, mh:mh + 4, :].rearrange("p m (s h) -> p m s h", s=8)

                ecS, esS = kview(EC, 14, 6, -1), kview(ES, 14, 6, -1)
                table(ov(tabs["WSTre"]), ("tab", "WSTre", st), ecS, bview(BbR), esS, bview(BbI), "sub")
                table(ov(tabs["WSTim"]), ("tab", "WSTim", st), esS, bview(BbR), ecS, bview(BbI), "add")
                ecN, esN = kview(EC, 0, 8, 1), kview(ES, 0, 8, 1)
                table(ov(tabs["WOnre"]), ("tab", "WOnre", st), ecN, bview(CR), esN, bview(CI), "sub")
                table(ov(tabs["WOnim"]), ("tab", "WOnim", st), esN, bview(CR), ecN, bview(CI), "negadd")
                ecP, esP = kview(EC, 8, 16, 1), kview(ES, 8, 16, 1)
                table(ov(WOre), ("tab", "WOre"), ecP, bview(CR), esP, bview(CI), "sub")
                table(ov(WOim), ("tab", "WOim"), esP, bview(CR), ecP, bview(CI), "negadd")
            return E

        for fn in gen(0):
            fn()
        for gb in range(8):
            m0 = 8 * gb
            st = gb % 2
            tabs = tabsets[st]
            self.dma("sp", self.tb_wo[gb], WOpair, r=[("tab", "WOre"), ("tab", "WOim")], w=[("tbwo", gb)])
            nxt = gen(gb + 1) if gb < 7 else []
            nsl = 24
            per = (len(nxt) + nsl - 1) // nsl
            pos = [0]

            def pump():
                for fn in nxt[pos[0]:pos[0] + per]:
                    fn()
                pos[0] += per

            tabkeys = [("tab", nm, st) for nm in ("WSTre", "WSTim", "WOnre", "WOnim")]
            for ml in range(8):
                pump()
                for ri, nm in enumerate(("WSTre", "WSTim")):
                    pb = ri
                    tp = self.ps(pb).bitcast(BF16)[:, 0:128]
                    self.op("pe", lambda e: e.transpose(out=tp, in_=tabs[nm][:, ml, :], identity=self.ident_bf), r=[("tab", nm, st), "ident"], w=[self.pskey(pb)])
                    dst = tabs["WS2re" if ri == 0 else "WS2im"][:, ml, :]
                    self.op("act", lambda e: e.copy(out=dst, in_=tp), r=[self.pskey(pb)], w=[("ws2", ri, st)])
                for par in range(2):
                    gl = 2 * ml + par
                    pb = 2 + par
                    lo, hi = 64 * par, 64 * par + 64
                    self.op("pe", lambda e: e.matmul(self.ps(pb)[:, 0:128], lhsT=tabs["WSTre"][lo:hi, ml, :], rhs=tabs["WOnre"][lo:hi, ml, :], start=True, stop=False),
                            r=tabkeys, w=[self.pskey(pb)])
                    self.op("pe", lambda e: e.matmul(self.ps(pb)[:, 0:128], lhsT=tabs["WSTim"][lo:hi, ml, :], rhs=tabs["WOnim"][lo:hi, ml, :], start=False, stop=True),
                            r=tabkeys, w=[self.pskey(pb)])
                    tg = TOEP[:, gl, :]
                    self.op("dve", lambda e: e.tensor_tensor(out=tg, in0=self.ps(pb)[:, 0:128], in1=self.cmask, op=ALU.mult), r=[self.pskey(pb), "cst"], w=["toep"])
                    gcolumn = self.prm[:, DD0 + l * 128 + 16 * gb + gl:DD0 + l * 128 + 16 * gb + gl + 1]
                    self.op("dve", lambda e: e.scalar_tensor_tensor(out=tg, in0=self.ident_bf, scalar=gcolumn, in1=tg, op0=ALU.mult, op1=ALU.add),
                            r=["toep", "prm", "ident"], w=["toep"])
            self.dma("sp", self.tb_toep[gb], TOEP, r=["toep"], w=[("tbtoep", gb)])
            for T in (2 * gb, 2 * gb + 1):
                for gl in range(8):
                    pump()
                    g = 8 * T + gl
                    pb = 4 + (g % 2)
                    for j in range(8):
                        self.op("pe", lambda e: e.matmul(self.ps(pb)[:, 0:128], lhsT=ZP[:, gl, 112 - 16 * j:240 - 16 * j], rhs=Uf[:, T, j:1024:8], start=(j == 0), stop=(j == 7)),
                                r=["sel", ("P1", T)], w=[self.pskey(pb)])
                    self.op("act", lambda e: e.copy(out=UG[:, g, :], in_=self.ps(pb)[:, 0:128]), r=[self.pskey(pb)], w=[("UG", g)])
                for mq in range(4):
                    m = 4 * T + mq
                    ml = m - m0
                    for ri in range(2):
                        pb = 6 + ri
                        ws2 = tabs["WS2re" if ri == 0 else "WS2im"]
                        for par in range(2):
                            lo, hi = 64 * par, 64 * par + 64
                            self.op("pe", lambda e: e.matmul(self.ps(pb)[lo:hi, 0:128], lhsT=ws2[:, ml, lo:hi], rhs=UG[:, 2 * m + par, :], start=True, stop=True),
                                    r=[("ws2", ri, st), ("UG", 2 * m + par)], w=[self.pskey(pb)])
                        self.op("act", lambda e: e.copy(out=SX[:, m, ri, :], in_=self.ps(pb)[:, 0:128]), r=[self.pskey(pb)], w=[("P1", T), ("SX", m)])
            while pos[0] < len(nxt):
                pump()
        self.barrier()
        if CUT == 52:
            return
        self.dump("UG", P2[:, 0:16384], [])
        self.dump("S", P1[:, 0:16384], [])

        NSEG, SL = 4, 32
        o = T_OFF
        XS = [self.sb([128, NSEG, 64, 2], F32, o + i * 2048) for i in range(2)]; o += 4096
        W1 = self.sb([128, NSEG, 64, 2], F32, o); o += 2048
        W2 = self.sb([128, NSEG, 64, 2], F32, o); o += 2048
        MR = self.sb([128, 64], F32, o); o += 256
        MI = self.sb([128, 64], F32, o); o += 256
        MI2 = self.sb([128, 64, 2], F32, o); o += 512
        q1 = self.sb([128, 64], F32, o); o += 256
        q2 = self.sb([128, 64], F32, o); o += 256
        q3 = self.sb([128, 64], F32, o); o += 256
        ACC = self.sb([128, 64, 2], F32, o); o += 512
        c1_, c2_ = self.sb([128, 64, 2], F32, o), self.sb([128, 64, 2], F32, o + 512); o += 1024
        SXs = SX.rearrange("p m r (h j) -> p h m r j", h=NSEG)
        ARb4 = bc(AR.unsqueeze(1).unsqueeze(3), [128, NSEG, 64, 2])
        AI4 = bc(AI2.unsqueeze(1), [128, NSEG, 64, 2])

        def scan(store):
            for j in range(SL):
                xc, xn = XS[j % 2], XS[(j + 1) % 2]
                sc_ = SXs[:, :, :, :, j]
                tt_(W1, xc, ARb4, ALU.mult, [("X", j % 2), "AR"], ["W1"])
                tt_(W2, xc[:, :, :, ::-1], AI4, ALU.mult, [("X", j % 2), "AI2"], ["W2"])
                tt_(W1, W1, W2, ALU.add, ["W1", "W2"], ["W1"])
                tt_(xn, W1, sc_, ALU.add, ["W1", ("Sc", j)], [("X", (j + 1) % 2)])
                if store:
                    self.op("act", lambda e: e.copy(out=sc_, in_=xc), r=[("X", j % 2)], w=[("Sc", j)])

        def cmul_add(dst, src, add, mr_b, mi2, keys_r, key_w):
            tt_(c1_, src, mr_b, ALU.mult, keys_r, ["c1_"])
            tt_(c2_, src[:, :, ::-1], mi2, ALU.mult, keys_r, ["c2_"])
            tt_(c1_, c1_, c2_, ALU.add, ["c1_", "c2_"], ["c1_"])
            tt_(dst, c1_, add, ALU.add, ["c1_"] + keys_r, key_w)

        self.op("dve", lambda e: e.tensor_copy(out=MR, in_=AR), r=["AR"], w=["MR"])
        self.op("dve", lambda e: e.tensor_copy(out=MI, in_=AI2[:, :, 1]), r=["AI2"], w=["MI"])
        for _ in range(5):
            tt_(q1, MR, MR, ALU.mult, ["MR"], ["q1"])
            tt_(q2, MI, MI, ALU.mult, ["MI"], ["q2"])
            tt_(q3, MR, MI, ALU.mult, ["MR", "MI"], ["q3"])
            tt_(MR, q1, q2, ALU.subtract, ["q1", "q2"], ["MR"])
            self.op("dve", lambda e: e.tensor_scalar(out=MI, in0=q3, scalar1=2.0, scalar2=None, op0=ALU.mult), r=["q3"], w=["MI"])
        self.op("dve", lambda e: e.tensor_copy(out=MI2[:, :, 1], in_=MI), r=["MI"], w=["MI2"])
        self.op("dve", lambda e: e.tensor_scalar(out=MI2[:, :, 0], in0=MI, scalar1=-1.0, scalar2=None, op0=ALU.mult), r=["MI"], w=["MI2"])
        MRb = bc(MR.unsqueeze(2), [128, 64, 2])
        mk_ = ["MR", "MI2"]
        self.op("dve", lambda e: e.memset(XS[0], 0.0), w=[("X", 0)])
        scan(False)
        E = XS[SL % 2]
        ek = ("X", SL % 2)
        self.op("dve", lambda e: e.tensor_copy(out=ACC, in_=E[:, 0]), r=[ek], w=["ACC"])
        for h in range(1, NSEG):
            cmul_add(ACC, ACC, E[:, h], MRb, MI2, ["ACC", ek] + mk_, ["ACC"])
        accf = ACC.rearrange("p m r -> p (m r)")
        self.dma("sp", self.xs_in, accf, r=["ACC"], w=["xs_in"])
        self.allgather_pairs(self.xs_in, self.xs_out, r=["xs_in"], w=["xs_out"])
        self.dma("sp", accf, self.xs_out[0:128, :], r=["xs_out"], w=["ACC"])
        self.op("dve", lambda e: e.tensor_scalar(out=accf, in0=accf, scalar1=self.prm[:, ISSEC:ISSEC + 1], scalar2=None, op0=ALU.mult),
                r=["ACC", "prm"], w=["ACC"])
        Xst = XS[(SL + 1) % 2]
        sk_ = ("X", (SL + 1) % 2)
        self.op("dve", lambda e: e.tensor_copy(out=Xst[:, 0], in_=ACC), r=["ACC"], w=[sk_])
        for h in range(1, NSEG):
            cmul_add(Xst[:, h], Xst[:, h - 1], E[:, h - 1], MRb, MI2, [sk_, ek] + mk_, [sk_])
        if (SL + 1) % 2 != 0:
            self.op("dve", lambda e: e.tensor_copy(out=XS[0], in_=Xst), r=[sk_], w=[("X", 0)])
        scan(True)
        self.barrier()
        if CUT == 53:
            return
        self.dump("XP", P1[:, 0:16384], [])

        TB = [self.sb([128, 16, 128], BF16, T_OFF + i * 8192) for i in range(2)]
        WB = [self.sb([128, 2, 8, 128], BF16, T_OFF + i * 8192 + 4096) for i in range(2)]
        YG = [self.sb([128, 8, 128], BF16, T_OFF + 16384 + i * 2048) for i in range(2)]
        GT = self.sb([128, 1024], F32, T_OFF + 20480)
        nt = 0
        for gb in range(8):
            s = gb % 2
            self.dma("sp", TB[s], self.tb_toep[gb], w=[("TB", s)])
            self.dma("sp", WB[s], self.tb_wo[gb], w=[("WB", s)])
            for T in (2 * gb, 2 * gb + 1):
                yg, ygk = YG[nt % 2], ("YG", nt % 2)
                for gl in range(8):
                    g = 8 * T + gl
                    m, par = g // 2, g % 2
                    lo, hi = 64 * par, 64 * par + 64
                    pb = (0, 1, 6, 7)[g % 4]
                    self.op("pe", lambda e: e.matmul(self.ps(pb)[:, 0:128], lhsT=TB[s][:, g - 16 * gb, :], rhs=UG[:, g, :], start=True, stop=False),
                            r=[("TB", s), ("P2", T)], w=[self.pskey(pb)])
                    self.op("pe", lambda e: e.matmul(self.ps(pb)[:, 0:128], lhsT=WB[s][lo:hi, 0, m - 8 * gb, :], rhs=SX[lo:hi, m, 0, :], start=False, stop=False),
                            r=[("WB", s)], w=[self.pskey(pb)])
                    self.op("pe", lambda e: e.matmul(self.ps(pb)[:, 0:128], lhsT=WB[s][lo:hi, 1, m - 8 * gb, :], rhs=SX[lo:hi, m, 1, :], start=False, stop=True),
                            r=[("WB", s)], w=[self.pskey(pb)])
                    self.op("act", lambda e: e.copy(out=yg[:, gl, :], in_=self.ps(pb)[:, 0:128]), r=[self.pskey(pb)], w=[ygk])
                po = self.ps2[1 + nt % 2]
                pk = [self.pskey(2 + 2 * (nt % 2)), self.pskey(3 + 2 * (nt % 2))]
                for t in range(8):
                    for gl in range(8):
                        c0 = 112 + 256 * t - 16 * gl
                        self.op("pe", lambda e: e.matmul(po[:, t * 128:(t + 1) * 128], lhsT=ZZ[:, c0:c0 + 128], rhs=yg[:, gl, :], start=(gl == 0), stop=(gl == 7)),
                                r=["sel", ygk], w=pk)
                for hf in range(2):
                    pkh = [pk[hf]]
                    ph = po[:, hf * 512:(hf + 1) * 512]
                    gt = GT[:, hf * 512:(hf + 1) * 512]
                    gk = ("GT", hf)
                    self.op("act", lambda e: e.activation(out=gt, in_=ph, func=AF.Square), r=pkh, w=[gk] + pkh)
                    self.op("dve", lambda e: e.tensor_scalar(out=gt, in0=gt, scalar1=0.044715, scalar2=1.0, op0=ALU.mult, op1=ALU.add), r=[gk], w=[gk])
                    self.op("dve", lambda e: e.tensor_tensor(out=gt, in0=gt, in1=ph, op=ALU.mult), r=[gk] + pkh, w=[gk] + pkh)
                    self.op("act", lambda e: e.activation(out=gt, in_=gt, func=AF.Sigmoid, scale=1.5957691216057308), r=[gk], w=[gk])
                    gout = G_[:, T, :].rearrange("p (c t) -> p t c", t=8)[:, 4 * hf:4 * hf + 4, :]
                    self.op("dve", lambda e: e.tensor_tensor(out=gout, in0=gt.rearrange("p (t c) -> p t c", t=4), in1=ph.rearrange("p (t c) -> p t c", t=4), op=ALU.mult),
                            r=[gk] + pkh, w=[("P2", T)] + pkh)
                nt += 1
        self.barrier()
        if CUT == 54:
            return
        self.dump("G", P2[:, 0:16384], [])
        for i in range(16):
            slot, wk = self.wload(self.d_wglu[l][i])
            for b in range(2):
                pv, pg = (4 * i + 2 * b) % 6, (4 * i + 2 * b) % 6 + 1
                for k in range(KT):
                    self.op("pe", lambda e: e.matmul(self.ps(pv), lhsT=slot[:, k, 0:128], rhs=G_[:, k, b * 512:(b + 1) * 512], start=(k == 0), stop=(k == KT - 1)),
                            r=[wk], w=[self.pskey(pv)])
                for k in range(KT):
                    self.op("pe", lambda e: e.matmul(self.ps(pg), lhsT=slot[:, k, 128:256], rhs=G_[:, k, b * 512:(b + 1) * 512], start=(k == 0), stop=(k == KT - 1)),
                            r=[wk], w=[self.pskey(pg)])
                sg, mf = self.ctmp[0], self.ctmp[1]
                self.op("act", lambda e: e.activation(out=sg, in_=self.ps(pg), func=AF.Sigmoid), r=[self.pskey(pg)], w=[("ctmp", 0)])
                self.op("dve", lambda e: e.tensor_tensor(out=mf, in0=self.ps(pv), in1=sg, op=ALU.mult), r=[self.pskey(pv), ("ctmp", 0)], w=[("ctmp", 1)])
                self.ss_acc_b(b, mf, [("ctmp", 1)], i == 0, i == 15)
                self.op("dve", lambda e: e.tensor_copy(out=M[:, i, b * 512:(b + 1) * 512], in_=mf), r=[("ctmp", 1)], w=[("M", i, b)])
        for b in range(2):
            self.post_apply_b(b, gcol(l, G_MIX1), lambda k: M[:, k, b * 512:(b + 1) * 512], lambda k: ("M", k, b))
        self.barrier()

    def store_out(self):
        for k in range(KT):
            self.dma("sp", self.d_out[k * 128:(k + 1) * 128, :], self.xT[:, k, :], r=[("x", k, 0), ("x", k, 1)], w=[("out", k)])

    def build(self):
        self.setup()
        for st in self.stages:
            if st == "memprep":
                self.mem_prepare()
            elif st[0] == "mem":
                self.mem_attn(st[1])
            elif st[0] == "ffn":
                self.ffn(st[1])
            elif st[0] == "mixa":
                self.mixer_a(st[1])
            elif st[0] == "kv":
                self.kv_phase()
            elif st[0] == "mixb":
                self.mixer_b(st[1])
        self.store_out()
        self.barrier()


def _tile_w(W, cw, nk=None):
    din, dout = W.shape
    return np.ascontiguousarray(W.reshape(din // 128, 128, dout // cw, cw).transpose(2, 1, 0, 3))


def _gain16(v):
    return v.reshape(16, 128).T


_CACHE = {}


def _consts():
    if "c" in _CACHE:
        return _CACHE["c"]
    cst = np.zeros((128, 288), np.float32)
    cst[:, 0:128] = np.eye(128, dtype=np.float32)
    r = np.arange(128)
    cst[:, 128:256] = ((r[None, :] // 16) >= (r[:, None] // 16)).astype(np.float32)
    cst[:, 256:272] = np.arange(-7, 9, dtype=np.float32)[None, :]
    cst[:, 272] = PI / 2 + 65 * PI
    cst[:, 273] = 65 * PI
    q = np.arange(128)[:, None]
    c = np.arange(640)[None, :]
    ok = np.where(q < 64, c < 576, c >= 64)
    maskT = np.where(ok, 0.0, -1e30).astype(np.float32)
    idx = np.clip(q - c + 512, -63, 256) + 63
    zpad = np.zeros((128, 8, 240), np.float32)
    for gl in range(8):
        for h in range(16):
            zpad[16 * gl + h, gl, 112 + h] = 1.0
    zz = np.zeros((128, 2160), np.float32)
    for t in range(8):
        for h in range(16):
            zz[16 * t + h, 112 + 256 * t + h] = 1.0
    selc = np.concatenate([zpad.reshape(128, 1920), zz], axis=1)
    _CACHE["c"] = (cst, maskT, idx, selc)
    return _CACHE["c"]


def prep_inputs(inp, names):
    f = lambda a: np.ascontiguousarray(np.asarray(a, dtype=np.float32))
    cst, maskT, idx, selc = _consts()
    names = set(names)
    sh = {}
    sh["cst"] = cst
    sh["maskT"] = maskT
    sh["selc"] = selc
    prm = np.zeros((128, NPRM), np.float32)
    for l in range(DEPTH):
        for w_, name in ((G_MIX0, "norm_mix"), (G_MEM0, "norm_mem"), (G_FFN0, "norm_ffn")):
            for s in range(2):
                c0 = gcol(l, w_ + s)
                prm[:, c0:c0 + 16] = _gain16(f(inp[name])[l, s])
        cwv = f(inp["f_conv_w"])[l]
        for tap in range(3):
            prm[:, CONV0 + l * 352 + tap * 88:CONV0 + l * 352 + tap * 88 + 88] = cwv[tap].reshape(88, 128).T
        prm[:, CONV0 + l * 352 + 3 * 88:CONV0 + l * 352 + 4 * 88] = f(inp["f_conv_b"])[l].reshape(88, 128).T
    prm[:, G_MEMIN:G_MEMIN + 16] = _gain16(f(inp["mem_in_norm"]))
    prm[:, G_KV:G_KV + 16] = _gain16(f(inp["kv_norm"]))
    for l in range(NA):
        d = f(inp["a_d"])[l].reshape(128, 16)
        prm[:, DD0 + l * 128:DD0 + (l + 1) * 128] = np.tile(d.T, (8, 1))
    for l in range(NA):
        if "s5lam_%d" % l in names:
            s5lam = np.zeros((128, 3, 64), np.float32)
            s5bc = np.zeros((4, 128, 64, 16), np.float32)
            lr = f(inp["a_lam_re"])[l].reshape(64, 2, 64)
            li = f(inp["a_lam_im"])[l].reshape(64, 2, 64)
            ld = f(inp["a_log_dt"])[l].reshape(64, 2)
            s5lam[:, 0, :] = lr.transpose(1, 2, 0).reshape(128, 64)
            s5lam[:, 1, :] = li.transpose(1, 2, 0).reshape(128, 64)
            s5lam[:, 2, :] = np.broadcast_to(ld.T[:, None, :], (2, 64, 64)).reshape(128, 64)
            br = f(inp["a_b_re"])[l].reshape(64, 2, 64, 16)
            bi = f(inp["a_b_im"])[l].reshape(64, 2, 64, 16)
            cr = f(inp["a_c_re"])[l].reshape(64, 2, 16, 64)
            ci = f(inp["a_c_im"])[l].reshape(64, 2, 16, 64)
            s5bc[0] = br.transpose(1, 2, 0, 3).reshape(128, 64, 16)
            s5bc[1] = bi.transpose(1, 2, 0, 3).reshape(128, 64, 16)
            s5bc[2] = cr.transpose(1, 3, 0, 2).reshape(128, 64, 16)
            s5bc[3] = ci.transpose(1, 3, 0, 2).reshape(128, 64, 16)
            sh["s5lam_%d" % l] = s5lam
            sh["s5bc_%d" % l] = s5bc
            sh["a_w_in_%d" % l] = _tile_w(f(inp["a_w_in"][l]), 128)
            wg = f(inp["a_w_glu"][l])
            sh["a_w_glu_%d" % l] = np.concatenate([_tile_w(wg[:, :D], 128), _tile_w(wg[:, D:], 128)], axis=3)
    if "w_k" in names:
        sh["w_k"] = _tile_w(f(inp["w_k"]), 128)
        sh["w_v"] = _tile_w(f(inp["w_v"]), 256)
    for l in range(NA, DEPTH):
        if "b_w_q_%d" % l in names:
            j = l - NA
            sh["b_w_q_%d" % l] = _tile_w(f(inp["b_w_q"][j]), 128)
            sh["b_w_o_%d" % l] = _tile_w(f(inp["b_w_o"][j]), 128)
            sh["b_bias_%d" % l] = np.ascontiguousarray(f(inp["b_rel_bias"][j])[:, idx])
    for l in range(DEPTH):
        if "m_w_q_%d" % l in names:
            sh["m_w_q_%d" % l] = _tile_w(f(inp["m_w_q"][l]), 128)
            mkv = f(inp["m_w_kv"][l])
            sh["m_w_k_%d" % l] = _tile_w(mkv[:, :512], 128)
            sh["m_w_v_%d" % l] = _tile_w(mkv[:, 512:], 256)
            sh["m_w_o_%d" % l] = _tile_w(f(inp["m_w_o"][l]), 128)
        if "f_w_up_%d" % l in names:
            wu = f(inp["f_w_up"][l])
            sh["f_w_up_%d" % l] = np.concatenate([_tile_w(wu[:, :DFF], 128), _tile_w(wu[:, DFF:], 128)], axis=3)
            wd = f(inp["f_w_down"][l])
            sh["f_w_down_%d" % l] = np.ascontiguousarray(wd.reshape(2, 22, 128, 16, 128).transpose(3, 0, 2, 1, 4))
    x = f(inp["x"])
    mem = f(inp["mem"])
    maps = []
    for c in range(8):
        b, half = c // 2, c % 2
        m = dict(sh)
        m["xT"] = np.ascontiguousarray(x[b, half * NTOK:(half + 1) * NTOK, :].T)
        m["memT"] = np.ascontiguousarray(mem[b].T)
        p = prm.copy()
        p[:, ISSEC] = float(half)
        m["prm"] = p
        hm = np.zeros((1, 1152), np.float32)
        if half == 0:
            hm[0, :512] = -1e30
        m["hmask"] = hm
        maps.append({k: v for k, v in m.items() if k in names})
    return maps


FULL_STAGES = ["memprep"]
for _l in range(DEPTH):
    if _l == NA:
        FULL_STAGES.append(("kv",))
    FULL_STAGES.append(("mixa", _l) if _l < NA else ("mixb", _l))
    FULL_STAGES.append(("mem", _l))
    FULL_STAGES.append(("ffn", _l))


def run(inputs, stages, dumps=(), trace=False):
    prog = Prog(stages, dumps)
    maps = prep_inputs(inputs, prog.in_names)
    res = run_bass_kernel_spmd(prog.nc, maps, core_ids=list(range(8)), trace=trace)
    out = np.zeros((4, 2 * NTOK, D), np.float32)
    for c in range(8):
        b, half = c // 2, c % 2
        out[b, half * NTOK:(half + 1) * NTOK, :] = res.results[c]["outT"].T
    return out, res


def kernel(**inputs):
    out, _ = run(inputs, FULL_STAGES)
    return out
```
